# Optimizing a Trainium2 kernel written in Bass

```python
import math
import jax, jax.numpy as jnp
from jax import lax
import numpy as np

D_MODEL = 4096
BATCH = 4
SEQ = 4096
DEPTH = 1

ATTN_HEADS = 12
HEAD_DIM = 128
ATTN_WIDTH = ATTN_HEADS * 2 * HEAD_DIM
SSM_WIDTH = D_MODEL - ATTN_WIDTH
SSM_GROUP = 16
SSM_GROUPS = SSM_WIDTH // SSM_GROUP
SSM_STATE = 64
DT_MIN = 1e-3
DT_MAX = 1e-1
IN_WIDTH = 3 * ATTN_WIDTH + SSM_WIDTH
D_FF = 11008
CONV_WIDTH = 3
Q_BLOCK = 128
EPS = 1e-6

kernel_name = "hymba_diffattn_s5_convffn"


def rms_norm(x, g):
    xf = x.astype(jnp.float32)
    y = xf * lax.rsqrt(jnp.mean(xf * xf, axis=-1, keepdims=True) + EPS)
    return (y * g.astype(jnp.float32)).astype(x.dtype)


def diff_attention(q, k, v, lam, q_gain, k_gain, sub_gain, lam_init):
    b, s = q.shape[:2]
    nb = s // Q_BLOCK
    q = rms_norm(q, q_gain)
    k = rms_norm(k, k_gain)
    scale = HEAD_DIM ** -0.5
    qb = q.reshape(b, nb, Q_BLOCK, ATTN_HEADS, 2, HEAD_DIM).transpose(1, 0, 3, 4, 2, 5)
    kt = k.transpose(0, 2, 3, 1, 4)
    vt = v.transpose(0, 2, 1, 3)
    k_pos = jnp.arange(s)

    def block(args):
        q_blk, i = args
        scores = jnp.einsum('bhmqd,bhmkd->bhmqk', q_blk, kt).astype(jnp.float32) * scale
        q_pos = i * Q_BLOCK + jnp.arange(Q_BLOCK)
        mask = k_pos[None, :] <= q_pos[:, None]
        scores = jnp.where(mask, scores, -jnp.inf)
        p = jax.nn.softmax(scores, axis=-1)
        w = p[:, :, 0] - lam * p[:, :, 1]
        return jnp.einsum('bhqk,bhke->bhqe', w.astype(v.dtype), vt)

    out = lax.map(block, (qb, jnp.arange(nb)))
    out = out.transpose(1, 0, 3, 2, 4).reshape(b, s, ATTN_HEADS, 2 * HEAD_DIM)
    out = rms_norm(out, sub_gain) * (1.0 - lam_init)
    return out.reshape(b, s, ATTN_WIDTH)


def s5_ssm(u, log_dt, a_re, a_im, b_re, b_im, c_re, c_im, d_skip, w_glu, b_glu, out_gain):
    f32 = jnp.float32
    bsz, s = u.shape[:2]
    ug = u.reshape(bsz, s, SSM_GROUPS, SSM_GROUP).astype(f32)
    dt = jnp.exp(log_dt.astype(f32))[:, None]
    ar, ai = a_re.astype(f32), a_im.astype(f32)
    mag = jnp.exp(ar * dt)
    lb_re = mag * jnp.cos(ai * dt)
    lb_im = mag * jnp.sin(ai * dt)
    pr, pi_ = lb_re - 1.0, lb_im
    den = ar * ar + ai * ai
    z_re = ((pr * ar + pi_ * ai) / den)[..., None]
    z_im = ((pi_ * ar - pr * ai) / den)[..., None]
    br, bi = b_re.astype(f32), b_im.astype(f32)
    bb_re = z_re * br - z_im * bi
    bb_im = z_re * bi + z_im * br
    bu_re = jnp.einsum('gnc,bsgc->bsgn', bb_re, ug)
    bu_im = jnp.einsum('gnc,bsgc->bsgn', bb_im, ug)
    la_re = jnp.broadcast_to(lb_re, bu_re.shape)
    la_im = jnp.broadcast_to(lb_im, bu_im.shape)

    def combine(e1, e2):
        a1r, a1i, b1r, b1i = e1
        a2r, a2i, b2r, b2i = e2
        return (a2r * a1r - a2i * a1i,
                a2r * a1i + a2i * a1r,
                a2r * b1r - a2i * b1i + b2r,
                a2r * b1i + a2i * b1r + b2i)

    _, _, h_re, h_im = lax.associative_scan(combine, (la_re, la_im, bu_re, bu_im), axis=1)
    y = (jnp.einsum('gcn,bsgn->bsgc', c_re.astype(f32), h_re)
         - jnp.einsum('gcn,bsgn->bsgc', c_im.astype(f32), h_im)
         + d_skip.astype(f32) * ug)
    y = jax.nn.gelu(y.reshape(bsz, s, SSM_WIDTH))
    y = y * jax.nn.sigmoid(y @ w_glu.astype(f32) + b_glu.astype(f32))
    return rms_norm(y.astype(u.dtype), out_gain)


def conv_glu_ffn(h, w_gate, w_up, conv_w, conv_b, w_down):
    s = h.shape[1]
    g = h @ w_gate
    up = h @ w_up
    gp = jnp.pad(g, ((0, 0), (CONV_WIDTH - 1, 0), (0, 0)))
    gc = conv_b + conv_w[0] * gp[:, 0:s]
    for j in range(1, CONV_WIDTH):
        gc = gc + conv_w[j] * gp[:, j:j + s]
    return (jax.nn.silu(gc) * up) @ w_down


def setup_inputs(seed: int = 0) -> dict:
    key = jax.random.key(seed)
    ks = jax.random.split(key, 32)
    L, D, F = DEPTH, D_MODEL, D_FF
    G, N, C = SSM_GROUPS, SSM_STATE, SSM_GROUP
    nrm = lambda k, shape, s: jax.random.normal(k, shape, jnp.float32) * s
    n_idx = jnp.arange(N, dtype=jnp.float32)
    return {
        "x": nrm(ks[0], (BATCH, SEQ, D), 1.0),
        "attn_norm": 1.0 + nrm(ks[1], (L, D), 0.02),
        "w_in": nrm(ks[2], (L, D, IN_WIDTH), D ** -0.5),
        "q_gain": 1.0 + nrm(ks[3], (L, HEAD_DIM), 0.02),
        "k_gain": 1.0 + nrm(ks[4], (L, HEAD_DIM), 0.02),
        "lam_q1": nrm(ks[5], (L, HEAD_DIM), 0.1),
        "lam_k1": nrm(ks[6], (L, HEAD_DIM), 0.1),
        "lam_q2": nrm(ks[7], (L, HEAD_DIM), 0.1),
        "lam_k2": nrm(ks[8], (L, HEAD_DIM), 0.1),
        "sub_gain": 1.0 + nrm(ks[9], (L, 2 * HEAD_DIM), 0.02),
        "ssm_log_dt": jax.random.uniform(ks[10], (L, G), jnp.float32, math.log(DT_MIN), math.log(DT_MAX)),
        "ssm_a_re": -0.5 + nrm(ks[11], (L, G, N), 0.01),
        "ssm_a_im": jnp.pi * n_idx + nrm(ks[12], (L, G, N), 0.01),
        "ssm_b_re": nrm(ks[13], (L, G, N, C), (2 * C) ** -0.5),
        "ssm_b_im": nrm(ks[14], (L, G, N, C), (2 * C) ** -0.5),
        "ssm_c_re": nrm(ks[15], (L, G, C, N), (2 * N) ** -0.5),
        "ssm_c_im": nrm(ks[16], (L, G, C, N), (2 * N) ** -0.5),
        "ssm_d": nrm(ks[17], (L, G, C), 1.0),
        "ssm_w_glu": nrm(ks[18], (L, SSM_WIDTH, SSM_WIDTH), SSM_WIDTH ** -0.5),
        "ssm_b_glu": nrm(ks[19], (L, SSM_WIDTH), 0.01),
        "ssm_out_gain": 1.0 + nrm(ks[20], (L, SSM_WIDTH), 0.02),
        "w_out": nrm(ks[21], (L, D, D), D ** -0.5),
        "ffn_norm": 1.0 + nrm(ks[22], (L, D), 0.02),
        "w_gate": nrm(ks[23], (L, D, F), D ** -0.5),
        "w_up": nrm(ks[24], (L, D, F), D ** -0.5),
        "conv_w": nrm(ks[25], (L, CONV_WIDTH, F), CONV_WIDTH ** -0.5),
        "conv_b": nrm(ks[26], (L, F), 0.01),
        "w_down": nrm(ks[27], (L, F, D), F ** -0.5),
    }


def reference(x, attn_norm, w_in, q_gain, k_gain, lam_q1, lam_k1, lam_q2, lam_k2, sub_gain,
              ssm_log_dt, ssm_a_re, ssm_a_im, ssm_b_re, ssm_b_im, ssm_c_re, ssm_c_im, ssm_d,
              ssm_w_glu, ssm_b_glu, ssm_out_gain, w_out, ffn_norm, w_gate, w_up, conv_w, conv_b,
              w_down):
    b, s = x.shape[:2]
    for l in range(DEPTH):
        lam_init = 0.8 - 0.6 * math.exp(-0.3 * l)
        h = rms_norm(x, attn_norm[l])
        proj = h @ w_in[l]
        q = proj[..., :ATTN_WIDTH].reshape(b, s, ATTN_HEADS, 2, HEAD_DIM)
        k = proj[..., ATTN_WIDTH:2 * ATTN_WIDTH].reshape(b, s, ATTN_HEADS, 2, HEAD_DIM)
        v = proj[..., 2 * ATTN_WIDTH:3 * ATTN_WIDTH].reshape(b, s, ATTN_HEADS, 2 * HEAD_DIM)
        u = proj[..., 3 * ATTN_WIDTH:]
        lam = (jnp.exp(jnp.sum(lam_q1[l].astype(jnp.float32) * lam_k1[l].astype(jnp.float32)))
               - jnp.exp(jnp.sum(lam_q2[l].astype(jnp.float32) * lam_k2[l].astype(jnp.float32)))
               + lam_init)
        attn_out = diff_attention(q, k, v, lam, q_gain[l], k_gain[l], sub_gain[l], lam_init)
        ssm_out = s5_ssm(u, ssm_log_dt[l], ssm_a_re[l], ssm_a_im[l], ssm_b_re[l], ssm_b_im[l],
                         ssm_c_re[l], ssm_c_im[l], ssm_d[l], ssm_w_glu[l], ssm_b_glu[l],
                         ssm_out_gain[l])
        mixed = jnp.concatenate([attn_out, ssm_out.astype(attn_out.dtype)], axis=-1)
        x = x + mixed @ w_out[l]
        h2 = rms_norm(x, ffn_norm[l])
        x = x + conv_glu_ffn(h2, w_gate[l], w_up[l], conv_w[l], conv_b[l], w_down[l])
    return x
```

```python
import math
import numpy as np
import ml_dtypes
from contextlib import ExitStack
import concourse.bass as bass
import concourse.mybir as mybir
from concourse.bass_utils import run_bass_kernel_spmd

F32, BF16, I32 = mybir.dt.float32, mybir.dt.bfloat16, mybir.dt.int32
AF = mybir.ActivationFunctionType
ALU = mybir.AluOpType
EPS = 1e-6
PI = math.pi

RATE_SCALE = 1.08
CFG_FULL = dict(D=4096, H=12, G=64, F=11008, HALF=2048)


def derive(cfg):
    c = dict(cfg)
    c["AW"] = c["H"] * 256
    c["SW"] = c["G"] * 16
    assert c["AW"] + c["SW"] == c["D"]
    c["KD"] = c["D"] // 128
    c["INW"] = 3 * c["AW"] + c["SW"]
    c["NST"] = c["G"] // 2
    c["NKC"] = c["SW"] // 128
    c["NF"] = c["F"] // 128
    c["T"] = 2 * c["HALF"]
    c["QR0"] = c["HALF"] - 128
    c["qchunks"] = [(c["HALF"] - 128, 128)] + [(c["HALF"] + 512 * i, 512) for i in range(c["HALF"] // 512)]
    return c


class Eng:
    def __init__(self, nc, name, e):
        self.e = e
        self.name = name
        self.sem = nc.semaphore("es_" + name).__enter__()
        self.cnt = 0
        self.seen = {}


class Buf:
    def __init__(self, name, t):
        self.name = name
        self.t = t
        self.w = {}
        self.r = {}
        self.sem = None
        self.val = 0

    def __getitem__(self, idx):
        return self.t[idx]


def _upd(d, tok):
    k = id(tok[0])
    if k not in d or d[k][1] < tok[1]:
        d[k] = tok


class KB:
    def __init__(self, nc):
        self.nc = nc
        self.PE = Eng(nc, "pe", nc.tensor)
        self.ACT = Eng(nc, "act", nc.scalar)
        self.DVE = Eng(nc, "dve", nc.vector)
        self.POOL = Eng(nc, "pool", nc.gpsimd)
        self.SP = Eng(nc, "sp", nc.sync)
        self.engs = [self.PE, self.ACT, self.DVE, self.POOL, self.SP]
        self.nsem = 0
        self.uid = 0
        self.sempool = []

    def _wait(self, E, toks):
        for key, (sem, v) in toks.items():
            if E is self.PE and sem is self.PE.sem:
                continue
            if E.seen.get(key, 0) >= v:
                continue
            E.e.wait_ge(sem, v)
            E.seen[key] = v

    def _haz(self, E, r, w, p):
        for b in r:
            self._wait(E, b.w)
        for b in w:
            self._wait(E, b.w)
            self._wait(E, b.r)
        for b in p:
            self._wait(E, b.r)

    def _commit(self, tok, r, w, p):
        for b in r:
            _upd(b.r, tok)
        for b in w:
            b.w = {id(tok[0]): tok}
            b.r = {}
        for b in p:
            _upd(b.w, tok)
            b.r = {}

    def op(self, E, fn, r=(), w=(), p=()):
        self._haz(E, r, w, p)
        inst = fn()
        E.cnt += 1
        inst.then_inc(E.sem, 1)
        tok = (E.sem, E.cnt)
        self._commit(tok, r, w, p)
        return tok

    def dma(self, Q, out, in_, sb, r=(), w=(), p=()):
        self._haz(Q, r, w, p)
        if sb.sem is None:
            if self.sempool:
                sb.sem, sb.val = self.sempool.pop()
            else:
                self.nsem += 1
                sb.sem = self.nc.semaphore("ds%d" % self.nsem).__enter__()
        inst = Q.e.dma_start(out=out, in_=in_)
        sb.val += 16
        inst.then_inc(sb.sem, 16)
        tok = (sb.sem, sb.val)
        self._commit(tok, r, w, p)
        return tok

    def barrier(self, bufs):
        allt = {}
        for b in bufs:
            for t in b.w.values():
                _upd(allt, t)
            for t in b.r.values():
                _upd(allt, t)
        for E in self.engs:
            if E.cnt > 0:
                _upd(allt, (E.sem, E.cnt))
        for E in self.engs:
            self._wait(E, allt)

    def retire(self, bufs):
        for b in bufs:
            if b.sem is not None:
                self.sempool.append((b.sem, b.val))
                b.sem = None

    def name(self, s):
        self.uid += 1
        return "%s_%d" % (s, self.uid)


def build_nc(cfg):
    c = derive(cfg)
    D, H, G, F, HALF = c["D"], c["H"], c["G"], c["F"], c["HALF"]
    AW, SW, KD, INW, NST, NKC, NF, T, QR0 = (c[k] for k in ("AW", "SW", "KD", "INW", "NST", "NKC", "NF", "T", "QR0"))
    qchunks = c["qchunks"]
    QRL = T - QR0
    NKT = T // 128
    QT0 = QR0 // 128

    nc = bass.Bass("TRN2", target_bir_lowering=False)
    kb = KB(nc)
    PE, ACT, DVE, POOL, SP = kb.PE, kb.ACT, kb.DVE, kb.POOL, kb.SP
    allbufs = []

    def din(name, shape, dt=F32):
        return Buf(name, nc.dram_tensor(name, list(shape), dt, kind="ExternalInput").ap())

    def dscr(name, shape, dt):
        b = Buf(name, nc.dram_tensor(name, list(shape), dt, kind="Internal").ap())
        allbufs.append(b)
        return b

    x_ext = din("x_ext", [T, D])
    flag_d = din("flag", [128, 1])
    anorm_d = din("anorm", [128, D])
    fnorm_d = din("fnorm", [128, D])
    w_in_d = din("w_in", [D, INW])
    w_out_d = din("w_out", [D, D])
    w_gate_d = din("w_gate", [D, F])
    w_up_d = din("w_up", [D, F])
    w_down_d = din("w_down", [F, D])
    w_glu_d = din("w_glu", [SW, SW])
    qg_d = din("qg", [128, 1])
    kg_d = din("kg", [128, 1])
    lamv_d = din("lamv", [128, 4, 128])
    subg_d = din("subg", [128, 256])
    sAre_d = din("sAre", [128, NST])
    sAim_d = din("sAim", [128, NST])
    sLdt_d = din("sLdt", [128, NST])
    cAre_d = din("cAre", [128, NKC * 64])
    cAim_d = din("cAim", [128, NKC * 64])
    cLdt_d = din("cLdt", [128, NKC * 64])
    BTre_d = din("BTre", [128, NKC * 64])
    BTim_d = din("BTim", [128, NKC * 64])
    CTre_d = din("CTre", [128, NST, 16])
    CTim_d = din("CTim", [128, NST, 16])
    dsk_d = din("dsk", [128, NKC])
    bglu_d = din("bglu", [128, NKC])
    outg_d = din("outg", [128, NKC])
    convw_d = din("convw", [128, 3, NF])
    convb_d = din("convb", [128, NF])
    ident_d = din("ident", [128, 128], BF16)
    cmask_d = din("cmask", [128, 128], BF16)
    negm_d = din("negm", [128, 128], BF16)
    onesb_d = din("onesb", [128, 128], BF16)
    iota_d = din("iota_t", [128, 128])
    rowmask_d = din("rowmask", [128, 8])
    y_out = Buf("y", nc.dram_tensor("y", [HALF, D], F32, kind="ExternalOutput").ap())
    allbufs.append(y_out)

    w_in_b = dscr("w_in_b", [D, INW], BF16)
    w_out_b = dscr("w_out_b", [D, D], BF16)
    w_gate_b = dscr("w_gate_b", [D, F], BF16)
    w_up_b = dscr("w_up_b", [D, F], BF16)
    w_down_b = dscr("w_down_b", [F, D], BF16)
    w_glu_b = dscr("w_glu_b", [SW, SW], BF16)
    kT_s = dscr("kT_s", [2 * H, 128, T], BF16)
    qT_s = dscr("qT_s", [2 * H, 128, T], BF16)
    uT_s = dscr("uT_s", [NKC, 128, T], BF16)
    vv_s = dscr("vv_s", [T, AW], BF16)
    mixT_s = dscr("mixT_s", [KD, 128, T], BF16)
    xmid_s = dscr("xmid_s", [T, D], F32)
    h2T_s = dscr("h2T_s", [KD, 128, T], BF16)
    actT_s = dscr("actT_s", [NF, 128, T], BF16)
    lhsB_s = dscr("lhsB_s", [128, NST, 2, 128], BF16)
    lhsC_s = dscr("lhsC_s", [128, NST, 2, 128], BF16)
    cosT_s = dscr("cosT_s", [128, NST, 128], F32)
    sinT_s = dscr("sinT_s", [128, NST, 128], F32)
    rho_s = dscr("rho_s", [128, NST], F32)
    e128_s = dscr("e128_s", [128, 2, NST], F32)
    y_s = dscr("y_s", [NKC, 128, T], F32)

    FB = [Buf("psf%d" % i, nc.alloc_psum_tensor("psf%d" % i, [128, 512], F32)) for i in range(7)]
    BB = [Buf("psb%d" % i, nc.alloc_psum_tensor("psb%d" % i, [128, 8, 128], BF16)) for i in range(1)]

    def cast_weight(src, dst, rows, cols):
        nchunk = max(1, (rows * cols) // (4 << 20))
        step = (rows + nchunk - 1) // nchunk
        r0 = 0
        while r0 < rows:
            r1 = min(rows, r0 + step)
            kb.dma(POOL, dst.t[r0:r1, :], src.t[r0:r1, :], sb=dst, p=[dst])
            r0 = r1

    regions = [("q", 0, AW), ("k", AW, AW), ("v", 2 * AW, AW), ("u", 3 * AW, SW)]
    w_in_tiles = {}
    for (kind, rs, rw) in regions[1:] + regions[:1]:
        c0 = rs
        while c0 < rs + rw:
            ncols = min(512, rs + rw - c0)
            tb_ = Buf("w_in_b_%d" % c0, w_in_b.t)
            allbufs.append(tb_)
            w_in_tiles[c0] = tb_
            kb.dma(POOL, w_in_b.t[:, c0:c0 + ncols], w_in_d.t[:, c0:c0 + ncols], sb=tb_, p=[tb_])
            c0 += ncols
    cast_weight(w_glu_d, w_glu_b, SW, SW)
    cast_weight(w_out_d, w_out_b, D, D)

    def sbp(name, shape, dt):
        b = Buf(name, nc.alloc_sbuf_tensor(kb.name(name), list(shape), dt))
        allbufs.append(b)
        return b

    ident = sbp("ident", [128, 128], BF16)
    cmask = sbp("cmask", [128, 128], BF16)
    negm = sbp("negm", [128, 128], BF16)
    onesb = sbp("onesb", [128, 128], BF16)
    flag = sbp("flag", [128, 1], F32)
    qgs = sbp("qgs", [128, 1], F32)
    kgs = sbp("kgs", [128, 1], F32)
    lamneg = sbp("lamneg", [128, 1], F32)
    subg = sbp("subg", [128, 256], F32)
    halfpi = sbp("halfpi", [128, 1], F32)
    epsb = sbp("epsb", [128, 1], F32)

    def load(dst, src):
        kb.dma(SP, dst.t[:], src.t[:], sb=dst, w=[dst])

    load(ident, ident_d)
    load(cmask, cmask_d)
    load(negm, negm_d)
    load(onesb, onesb_d)
    load(flag, flag_d)
    load(qgs, qg_d)
    load(kgs, kg_d)
    load(subg, subg_d)
    kb.op(DVE, lambda: nc.vector.memset(halfpi.t[:], PI / 2), w=[halfpi])
    kb.op(DVE, lambda: nc.vector.memset(epsb.t[:], EPS), w=[epsb])
    kb.op(DVE, lambda: nc.vector.tensor_scalar(out=qgs.t[:], in0=qgs.t[:], scalar1=128 ** -0.5, scalar2=None, op0=ALU.mult), w=[qgs])
    kb.op(DVE, lambda: nc.vector.tensor_scalar(out=subg.t[:], in0=subg.t[:], scalar1=0.8, scalar2=None, op0=ALU.mult), w=[subg])

    with ExitStack() as es:
        lamv = Buf("lamv", es.enter_context(nc.sbuf_tensor(kb.name("lamv"), [128, 4, 128], F32)))
        lj = Buf("lj", es.enter_context(nc.sbuf_tensor(kb.name("lj"), [128, 128], F32)))
        l2 = Buf("l2", es.enter_context(nc.sbuf_tensor(kb.name("l2"), [128, 2], F32)))
        load(lamv, lamv_d)
        for i in range(2):
            kb.op(DVE, lambda i=i: nc.vector.tensor_tensor(out=lj.t[:], in0=lamv.t[:, 2 * i, :], in1=lamv.t[:, 2 * i + 1, :], op=ALU.mult), r=[lamv], w=[lj])
            kb.op(DVE, lambda i=i: nc.vector.reduce_sum(out=l2.t[:, i:i + 1], in_=lj.t[:], axis=mybir.AxisListType.X), r=[lj], w=[l2])
        kb.op(ACT, lambda: nc.scalar.activation(out=l2.t[:], in_=l2.t[:], func=AF.Exp), w=[l2])
        kb.op(DVE, lambda: nc.vector.tensor_tensor(out=lamneg.t[:], in0=l2.t[:, 1:2], in1=l2.t[:, 0:1], op=ALU.subtract), r=[l2], w=[lamneg])
        kb.op(DVE, lambda: nc.vector.tensor_scalar(out=lamneg.t[:], in0=lamneg.t[:], scalar1=-0.2, scalar2=None, op0=ALU.add), w=[lamneg])
        kb.barrier([lamv, lj, l2])
        kb.retire([lamv, lj, l2])

    def rstd_from_ss(ss_ap, out_buf, out_ap, n, rbufs):
        kb.op(ACT, lambda: nc.scalar.activation(out=out_ap, in_=ss_ap, func=AF.Sqrt, scale=1.0 / n, bias=epsb.t[:, 0:1]), r=rbufs + [epsb], w=[out_buf])
        kb.op(DVE, lambda: nc.vector.reciprocal(out=out_ap, in_=out_ap), w=[out_buf])

    def transposes(src_buf, src_fn, n, dst_buf, dst_fn, idx0):
        k0 = 0
        i = idx0
        while k0 < n:
            cnt = min(8, n - k0)
            bank = BB[i % len(BB)]

            def f(k0=k0, cnt=cnt, bank=bank):
                inst = None
                for j in range(cnt):
                    inst = nc.tensor.transpose(out=bank.t[:, j, :], in_=src_fn(k0 + j), identity=ident.t[:])
                return inst
            kb.op(PE, f, r=[src_buf, ident], w=[bank])
            E = ACT if (i % 2 == 0) else DVE
            if E is ACT:
                kb.op(ACT, lambda k0=k0, cnt=cnt, bank=bank: nc.scalar.copy(out=dst_fn(k0, cnt), in_=bank.t[:, 0:cnt, :]), r=[bank], p=[dst_buf])
            else:
                kb.op(DVE, lambda k0=k0, cnt=cnt, bank=bank: nc.vector.tensor_copy(out=dst_fn(k0, cnt), in_=bank.t[:, 0:cnt, :]), r=[bank], p=[dst_buf])
            k0 += cnt
            i += 1
        return i

    with ExitStack() as es:
        def sb(name, shape, dt):
            return Buf(name, es.enter_context(nc.sbuf_tensor(kb.name(name), list(shape), dt)))
        NC64 = NKC * 64
        lhsB = sb("lhsB", [128, NST, 2, 128], BF16)
        lhsC = sb("lhsC", [128, NST, 2, 128], BF16)
        cosT = sb("cosT", [128, NST, 128], F32)
        sinT = sb("sinT", [128, NST, 128], F32)
        rho = sb("rho", [128, NST], F32)
        e128 = sb("e128", [128, 2, NST], F32)
        ph = [lhsB, lhsC, cosT, sinT, rho, e128]

        def sincos(es2, ang, n, s_out, c_out, sbuf_, cbuf_):
            def sb2(name, shape, dt):
                return Buf(name, es2.enter_context(nc.sbuf_tensor(kb.name(name), list(shape), dt)))
            ki = sb2("ki", [128, n], I32)
            kf = sb2("kf", [128, n], F32)
            sy = sb2("sy", [128, n], F32)
            cy = sb2("cy", [128, n], F32)
            kb.op(DVE, lambda: nc.vector.tensor_scalar(out=ki.t[:], in0=ang.t[:], scalar1=1.0 / (2 * PI), scalar2=None, op0=ALU.mult), r=[ang], w=[ki])
            kb.op(DVE, lambda: nc.vector.tensor_copy(out=kf.t[:], in_=ki.t[:]), r=[ki], w=[kf])
            kb.op(DVE, lambda: nc.vector.scalar_tensor_tensor(out=ang.t[:], in0=kf.t[:], scalar=-2 * PI, in1=ang.t[:], op0=ALU.mult, op1=ALU.add), r=[kf], w=[ang])
            kb.op(DVE, lambda: nc.vector.tensor_scalar(out=ang.t[:], in0=ang.t[:], scalar1=0.5, scalar2=None, op0=ALU.mult), w=[ang])
            kb.op(DVE, lambda: nc.vector.tensor_scalar(out=ang.t[:], in0=ang.t[:], scalar1=-3.1415925, scalar2=3.1415925, op0=ALU.max, op1=ALU.min), w=[ang])
            kb.op(ACT, lambda: nc.scalar.activation(out=sy.t[:], in_=ang.t[:], func=AF.Sin), r=[ang], w=[sy])
            kb.op(ACT, lambda: nc.scalar.activation(out=kf.t[:], in_=ang.t[:], func=AF.Abs), r=[ang], w=[kf])
            kb.op(ACT, lambda: nc.scalar.activation(out=cy.t[:], in_=kf.t[:], func=AF.Sin, scale=-1.0, bias=halfpi.t[:, 0:1]), r=[kf, halfpi], w=[cy])
            kb.op(DVE, lambda: nc.vector.scalar_tensor_tensor(out=s_out, in0=sy.t[:], scalar=2.0, in1=cy.t[:], op0=ALU.mult, op1=ALU.mult), r=[sy, cy], p=[sbuf_])
            kb.op(DVE, lambda: nc.vector.tensor_tensor(out=kf.t[:], in0=sy.t[:], in1=sy.t[:], op=ALU.mult), r=[sy], w=[kf])
            kb.op(DVE, lambda: nc.vector.tensor_scalar(out=c_out, in0=kf.t[:], scalar1=-2.0, scalar2=1.0, op0=ALU.mult, op1=ALU.add), r=[kf], p=[cbuf_])
            return [ki, kf, sy, cy]

        TT = lambda o, a, b, op: nc.vector.tensor_tensor(out=o, in0=a, in1=b, op=op)
        with ExitStack() as es2:
            def sb2(name, shape, dt):
                return Buf(name, es2.enter_context(nc.sbuf_tensor(kb.name(name), list(shape), dt)))
            cA = sb2("cA", [128, NC64], F32)
            cI = sb2("cI", [128, NC64], F32)
            cL = sb2("cL", [128, NC64], F32)
            bR = sb2("bR", [128, NC64], F32)
            bI = sb2("bI", [128, NC64], F32)
            t1 = sb2("t1", [128, NC64], F32)
            t2 = sb2("t2", [128, NC64], F32)
            t3 = sb2("t3", [128, NC64], F32)
            t4 = sb2("t4", [128, NC64], F32)
            t5 = sb2("t5", [128, NC64], F32)
            t6 = sb2("t6", [128, NC64], F32)
            rowm = sb2("rowm", [128, 8], F32)
            tmpb = [cA, cI, cL, bR, bI, t1, t2, t3, t4, t5, t6, rowm]
            load(cA, cAre_d)
            load(cI, cAim_d)
            load(cL, cLdt_d)
            load(bR, BTre_d)
            load(bI, BTim_d)
            load(rowm, rowmask_d)
            kb.op(ACT, lambda: nc.scalar.activation(out=cL.t[:], in_=cL.t[:], func=AF.Exp), w=[cL])
            kb.op(DVE, lambda: TT(t1.t[:], cA.t[:], cL.t[:], ALU.mult), r=[cA, cL], w=[t1])
            kb.op(ACT, lambda: nc.scalar.activation(out=t1.t[:], in_=t1.t[:], func=AF.Exp), w=[t1])
            kb.op(DVE, lambda: TT(t2.t[:], cI.t[:], cL.t[:], ALU.mult), r=[cI, cL], w=[t2])
            tmpb += sincos(es2, t2, NC64, t3.t[:], t4.t[:], t3, t4)
            kb.op(DVE, lambda: TT(t3.t[:], t3.t[:], t1.t[:], ALU.mult), r=[t1], w=[t3])
            kb.op(DVE, lambda: TT(t4.t[:], t4.t[:], t1.t[:], ALU.mult), r=[t1], w=[t4])
            kb.op(DVE, lambda: nc.vector.tensor_scalar(out=t4.t[:], in0=t4.t[:], scalar1=-1.0, scalar2=None, op0=ALU.add), w=[t4])
            kb.op(DVE, lambda: TT(t1.t[:], cA.t[:], cA.t[:], ALU.mult), r=[cA], w=[t1])
            kb.op(DVE, lambda: TT(t2.t[:], cI.t[:], cI.t[:], ALU.mult), r=[cI], w=[t2])
            kb.op(DVE, lambda: TT(t1.t[:], t1.t[:], t2.t[:], ALU.add), r=[t2], w=[t1])
            kb.op(DVE, lambda: nc.vector.reciprocal(out=t1.t[:], in_=t1.t[:]), w=[t1])
            kb.op(DVE, lambda: TT(t5.t[:], t4.t[:], cA.t[:], ALU.mult), r=[t4, cA], w=[t5])
            kb.op(DVE, lambda: TT(t2.t[:], t3.t[:], cI.t[:], ALU.mult), r=[t3, cI], w=[t2])
            kb.op(DVE, lambda: TT(t5.t[:], t5.t[:], t2.t[:], ALU.add), r=[t2], w=[t5])
            kb.op(DVE, lambda: TT(t5.t[:], t5.t[:], t1.t[:], ALU.mult), r=[t1], w=[t5])
            kb.op(DVE, lambda: TT(t6.t[:], t3.t[:], cA.t[:], ALU.mult), r=[t3, cA], w=[t6])
            kb.op(DVE, lambda: TT(t2.t[:], t4.t[:], cI.t[:], ALU.mult), r=[t4, cI], w=[t2])
            kb.op(DVE, lambda: TT(t6.t[:], t6.t[:], t2.t[:], ALU.subtract), r=[t2], w=[t6])
            kb.op(DVE, lambda: TT(t6.t[:], t6.t[:], t1.t[:], ALU.mult), r=[t1], w=[t6])
            kb.op(DVE, lambda: TT(t3.t[:], t5.t[:], bR.t[:], ALU.mult), r=[t5, bR], w=[t3])
            kb.op(DVE, lambda: TT(t2.t[:], t6.t[:], bI.t[:], ALU.mult), r=[t6, bI], w=[t2])
            kb.op(DVE, lambda: TT(t3.t[:], t3.t[:], t2.t[:], ALU.subtract), r=[t2], w=[t3])
            kb.op(DVE, lambda: TT(t4.t[:], t5.t[:], bI.t[:], ALU.mult), r=[t5, bI], w=[t4])
            kb.op(DVE, lambda: TT(t2.t[:], t6.t[:], bR.t[:], ALU.mult), r=[t6, bR], w=[t2])
            kb.op(DVE, lambda: TT(t4.t[:], t4.t[:], t2.t[:], ALU.add), r=[t2], w=[t4])
            for st in range(NST):
                kc = st // 4
                for gl in range(2):
                    j = (2 * st) % 8 + gl
                    for comp, src in ((0, t3), (1, t4)):
                        kb.op(DVE, lambda st=st, gl=gl, j=j, comp=comp, src=src, kc=kc: nc.vector.tensor_scalar(out=lhsB.t[:, st, comp, gl * 64:(gl + 1) * 64], in0=src.t[:, kc * 64:(kc + 1) * 64], scalar1=rowm.t[:, j:j + 1], scalar2=None, op0=ALU.mult), r=[src, rowm], p=[lhsB])
            kb.barrier(tmpb)
            kb.retire(tmpb)
        with ExitStack() as es2:
            def sb2(name, shape, dt):
                return Buf(name, es2.enter_context(nc.sbuf_tensor(kb.name(name), list(shape), dt)))
            sA = sb2("sA", [128, NST], F32)
            sI = sb2("sI", [128, NST], F32)
            sL = sb2("sL", [128, NST], F32)
            th = sb2("th", [128, NST], F32)
            a128 = sb2("a128", [128, NST], F32)
            iot = sb2("iot", [128, 128], F32)
            angT = sb2("angT", [128, NST * 128], F32)
            cTr = sb2("cTr", [128, NST, 16], F32)
            cTi = sb2("cTi", [128, NST, 16], F32)
            tmpb = [sA, sI, sL, th, a128, iot, angT, cTr, cTi]
            load(sA, sAre_d)
            load(sI, sAim_d)
            load(sL, sLdt_d)
            load(iot, iota_d)
            load(cTr, CTre_d)
            load(cTi, CTim_d)
            kb.op(ACT, lambda: nc.scalar.activation(out=sL.t[:], in_=sL.t[:], func=AF.Exp), w=[sL])
            kb.op(DVE, lambda: TT(rho.t[:], sA.t[:], sL.t[:], ALU.mult), r=[sA, sL], w=[rho])
            kb.op(ACT, lambda: nc.scalar.activation(out=rho.t[:], in_=rho.t[:], func=AF.Exp), w=[rho])
            kb.op(DVE, lambda: TT(th.t[:], sI.t[:], sL.t[:], ALU.mult), r=[sI, sL], w=[th])
            kb.op(DVE, lambda: nc.vector.tensor_scalar(out=a128.t[:], in0=th.t[:], scalar1=128.0, scalar2=None, op0=ALU.mult), r=[th], w=[a128])
            tmpb += sincos(es2, a128, NST, e128.t[:, 1, :], e128.t[:, 0, :], e128, e128)
            for st in range(NST):
                kb.op(DVE, lambda st=st: nc.vector.tensor_scalar(out=angT.t[:, st * 128:(st + 1) * 128], in0=iot.t[:], scalar1=th.t[:, st:st + 1], scalar2=None, op0=ALU.mult), r=[iot, th], p=[angT])
            tmpb += sincos(es2, angT, NST * 128, sinT.t[:].rearrange("p a b -> p (a b)"), cosT.t[:].rearrange("p a b -> p (a b)"), sinT, cosT)
            kb.op(DVE, lambda: nc.vector.memset(lhsC.t[:], 0.0), w=[lhsC])
            for st in range(NST):
                for gl in range(2):
                    j = (2 * st) % 8 + gl
                    kb.op(DVE, lambda st=st, gl=gl, j=j: nc.vector.tensor_copy(out=lhsC.t[gl * 64:(gl + 1) * 64, st, 0, j * 16:(j + 1) * 16], in_=cTr.t[gl * 64:(gl + 1) * 64, st, :]), r=[cTr], p=[lhsC])
                    kb.op(DVE, lambda st=st, gl=gl, j=j: nc.vector.tensor_scalar(out=lhsC.t[gl * 64:(gl + 1) * 64, st, 1, j * 16:(j + 1) * 16], in0=cTi.t[gl * 64:(gl + 1) * 64, st, :], scalar1=-1.0, scalar2=None, op0=ALU.mult), r=[cTi], p=[lhsC])
            kb.barrier(tmpb)
            kb.retire(tmpb)

        for (b_, d_) in ((lhsB, lhsB_s), (lhsC, lhsC_s), (cosT, cosT_s), (sinT, sinT_s), (rho, rho_s), (e128, e128_s)):
            kb.dma(POOL, d_.t[:], b_.t[:], sb=b_, r=[b_], p=[d_])
        kb.barrier(ph)
        kb.retire(ph)

    with ExitStack() as es:
        def sb(name, shape, dt):
            return Buf(name, es.enter_context(nc.sbuf_tensor(kb.name(name), list(shape), dt)))
        hT = sb("hT", [128, KD, 512], BF16)
        anorm = sb("anorm", [128, D], F32)
        xin = [sb("xin", [128, D], F32) for _ in range(2)]
        junk = sb("junk", [128, D], BF16)
        hb = [sb("hb", [128, D], BF16) for _ in range(2)]
        wt = [sb("wt", [128, KD, 512], BF16) for _ in range(2)]
        stg = [sb("stg", [128, 512], BF16) for _ in range(4)]
        sq = [sb("sq", [128, 512], BF16) for _ in range(2)]
        rinv = [sb("rinv", [128, 512], F32) for _ in range(2)]
        ssq = [sb("ssq", [128, 1], F32) for _ in range(2)]
        ph = [hT, anorm, junk] + xin + hb + wt + stg + sq + rinv + ssq
        load(anorm, anorm_d)

        n_x = n_tp = n_w = n_acc = n_stg = n_qk = 0
        for ch in range(T // 512):
            t0 = ch * 512
            for sub in range(4):
                xi = xin[n_x % 2]
                hbb = hb[n_x % 2]
                sqb = ssq[n_x % 2]
                n_x += 1
                r0 = t0 + sub * 128
                kb.dma(SP, xi.t[:], x_ext.t[r0:r0 + 128, :], sb=xi, w=[xi])
                kb.op(ACT, lambda xi=xi, sqb=sqb: nc.scalar.activation(out=junk.t[:], in_=xi.t[:], func=AF.Square, accum_out=sqb.t[:, 0:1]), r=[xi], w=[junk, sqb])
                rstd_from_ss(sqb.t[:, 0:1], sqb, sqb.t[:, 0:1], D, [sqb])
                kb.op(DVE, lambda xi=xi, sqb=sqb, hbb=hbb: nc.vector.scalar_tensor_tensor(out=hbb.t[:], in0=xi.t[:], scalar=sqb.t[:, 0:1], in1=anorm.t[:], op0=ALU.mult, op1=ALU.mult), r=[xi, sqb, anorm], w=[hbb])
                n_tp = transposes(hbb, lambda k, hbb=hbb: hbb.t[:, k * 128:(k + 1) * 128], KD, hT,
                                  lambda k0, cnt, sub=sub: hT.t[:, k0:k0 + cnt, sub * 128:(sub + 1) * 128], n_tp)
            for (kind, rs, rw) in regions:
                if kind == "q" and t0 + 512 <= QR0:
                    continue
                c0 = rs
                while c0 < rs + rw:
                    ncols = min(512, rs + rw - c0)
                    wtile = wt[n_w % 2]
                    n_w += 1
                    kb.dma(SP, wtile.t[:, :, 0:ncols], w_in_b.t[:, c0:c0 + ncols].rearrange("(k p) c -> p k c", p=128), sb=wtile, r=[w_in_tiles[c0]], w=[wtile])
                    if kind == "v":
                        for sub in range(4):
                            acc = FB[n_acc % 4]
                            n_acc += 1

                            def f(acc=acc, sub=sub, wtile=wtile, ncols=ncols):
                                inst = None
                                for k in range(KD):
                                    inst = nc.tensor.matmul(acc.t[:, 0:ncols], lhsT=hT.t[:, k, sub * 128:(sub + 1) * 128], rhs=wtile.t[:, k, 0:ncols], start=(k == 0), stop=(k == KD - 1))
                                return inst
                            kb.op(PE, f, r=[hT, wtile], w=[acc])
                            st_ = stg[n_stg % 4]
                            n_stg += 1
                            if n_stg % 2 == 0:
                                kb.op(ACT, lambda acc=acc, st_=st_, ncols=ncols: nc.scalar.copy(out=st_.t[:, 0:ncols], in_=acc.t[:, 0:ncols]), r=[acc], w=[st_])
                            else:
                                kb.op(DVE, lambda acc=acc, st_=st_, ncols=ncols: nc.vector.tensor_copy(out=st_.t[:, 0:ncols], in_=acc.t[:, 0:ncols]), r=[acc], w=[st_])
                            r0 = t0 + sub * 128
                            kb.dma(POOL, vv_s.t[r0:r0 + 128, c0 - rs:c0 - rs + ncols], st_.t[:, 0:ncols], sb=st_, r=[st_], p=[vv_s])
                    else:
                        for jb in range(ncols // 128):
                            blk = (c0 - rs) // 128 + jb
                            acc = FB[n_acc % 4]
                            n_acc += 1

                            def f(acc=acc, jb=jb, wtile=wtile):
                                inst = None
                                for k in range(KD):
                                    inst = nc.tensor.matmul(acc.t[:, :], lhsT=wtile.t[:, k, jb * 128:(jb + 1) * 128], rhs=hT.t[:, k, :], start=(k == 0), stop=(k == KD - 1))
                                return inst
                            kb.op(PE, f, r=[hT, wtile], w=[acc])
                            st_ = stg[n_stg % 4]
                            n_stg += 1
                            if kind == "u":
                                kb.op(ACT, lambda acc=acc, st_=st_: nc.scalar.copy(out=st_.t[:], in_=acc.t[:]), r=[acc], w=[st_])
                                kb.dma(POOL, uT_s.t[blk, :, t0:t0 + 512], st_.t[:], sb=st_, r=[st_], p=[uT_s])
                            else:
                                sqb = sq[n_qk % 2]
                                rb = rinv[n_qk % 2]
                                ssb = FB[4 + n_qk % 2]
                                n_qk += 1
                                kb.op(ACT, lambda acc=acc, sqb=sqb: nc.scalar.activation(out=sqb.t[:], in_=acc.t[:], func=AF.Square), r=[acc], w=[sqb])
                                kb.op(PE, lambda ssb=ssb, sqb=sqb: nc.tensor.matmul(ssb.t[:], lhsT=onesb.t[:], rhs=sqb.t[:], start=True, stop=True), r=[sqb, onesb], w=[ssb])
                                rstd_from_ss(ssb.t[:], rb, rb.t[:], 128, [ssb])
                                gn = qgs if kind == "q" else kgs
                                kb.op(DVE, lambda acc=acc, st_=st_, rb=rb, gn=gn: nc.vector.scalar_tensor_tensor(out=st_.t[:], in0=acc.t[:], scalar=gn.t[:, 0:1], in1=rb.t[:], op0=ALU.mult, op1=ALU.mult), r=[acc, rb, gn], w=[st_])
                                dst = qT_s if kind == "q" else kT_s
                                kb.dma(POOL, dst.t[blk, :, t0:t0 + 512], st_.t[:], sb=st_, r=[st_], p=[dst])
                    c0 += ncols
        kb.barrier(ph + allbufs + FB + BB)
        kb.retire(ph)

    with ExitStack() as es:
        def sb(name, shape, dt):
            return Buf(name, es.enter_context(nc.sbuf_tensor(kb.name(name), list(shape), dt)))
        TT = lambda o, a, b, op: nc.vector.tensor_tensor(out=o, in0=a, in1=b, op=op)
        GT = lambda o, a_, b_, op: nc.gpsimd.tensor_tensor(out=o, in0=a_, in1=b_, op=op)
        kTt = [sb("kTt", [128, 2, T], BF16) for _ in range(2)]
        Vt = [sb("Vt", [128, NKT, 257], BF16) for _ in range(1)]
        qTt = [sb("qTt", [128, 2, QRL], BF16) for _ in range(2)]
        pt = [sb("pt", [128, 512], BF16) for _ in range(3)]
        a1 = sb("a1", [128, 4, 256], F32)
        av = [sb("av", [128, 256], F32) for _ in range(2)]
        ab = [sb("ab", [128, 256], BF16) for _ in range(2)]
        aj = sb("aj", [128, 256], BF16)
        aTst = [sb("aTst", [128, 2, 512], BF16) for _ in range(2)]
        rl = [sb("rl", [128, 2], F32) for _ in range(4)]
        osb = [sb("osb", [128, 4, 257], F32) for _ in range(2)]
        ph = kTt + Vt + qTt + pt + [a1, aj] + av + ab + aTst + rl + osb
        NH = min(8, NST)
        NQ = NST // NH
        lhsB = sb("lhsB", [128, NST, 2, 128], BF16)
        lhsC = sb("lhsC", [128, NST, 2, 128], BF16)
        cosT = sb("cosT", [128, NST, 128], F32)
        sinT = sb("sinT", [128, NST, 128], F32)
        rho = sb("rho", [128, NST], F32)
        e128 = sb("e128", [128, 2, NST], F32)
        dsk = sb("dsk", [128, NKC], F32)
        ini = [sb("ini", [128, 2, NST], F32) for _ in range(2)]
        ut = [sb("ut", [128, NKC, 128], BF16) for _ in range(2)]
        vr = sb("vr", [128, NH, 128], F32)
        vi = sb("vi", [128, NH, 128], F32)
        gr = sb("gr", [128, NH, 128], F32)
        gi = sb("gi", [128, NH, 128], F32)
        hbr = [sb("hbr", [128, NH, 128], BF16) for _ in range(2)]
        hbi = [sb("hbi", [128, NH, 128], BF16) for _ in range(2)]
        pa = sb("pa", [128, 512], F32)
        pb2 = sb("pb2", [128, 512], F32)
        ta = sb("ta", [128, 128], F32)
        tb = sb("tb", [128, 128], F32)
        it4 = sb("it4", [128, 4, NH], F32)
        yst = [sb("yst", [128, NKC, 128], F32) for _ in range(2)]
        XBh = [FB[5], FB[6]]
        YB = FB[5]
        ph += [lhsB, lhsC, cosT, sinT, rho, e128, dsk, vr, vi, gr, gi, pa, pb2, ta, tb, it4] + ini + ut + hbr + hbi + yst
        for (b_, d_) in ((lhsB, lhsB_s), (lhsC, lhsC_s), (cosT, cosT_s), (sinT, sinT_s), (rho, rho_s), (e128, e128_s)):
            kb.dma(SP, b_.t[:], d_.t[:], sb=b_, r=[d_], w=[b_])
        load(dsk, dsk_d)
        for v_ in Vt:
            kb.op(DVE, lambda v_=v_: nc.vector.memset(v_.t[:, :, 256:257], 1.0), p=[v_])
            kb.op(DVE, lambda v_=v_: nc.vector.tensor_copy(out=v_.t[:, 0:HALF // 128, 256], in_=flag.t[:, 0:1].to_broadcast([128, HALF // 128])), r=[flag], p=[v_])

        def ssm_gen():
            kb.op(DVE, lambda: nc.vector.memset(ini[0].t[:], 0.0), w=[ini[0]])
            fifo = []
            nx = 0
            nq = 0
            for ti in range(T // 128):
                t0 = ti * 128
                u_ = ut[ti % 2]
                kb.dma(SP, u_.t[:], uT_s.t[:, :, t0:t0 + 128].rearrange("k p t -> p k t"), sb=u_, r=[uT_s], w=[u_])
                icur, inxt = ini[ti % 2], ini[(ti + 1) % 2]
                qtile = ti >= QT0
                yst_ = yst[ti % 2]
                for qr in range(NQ):
                    st0 = qr * NH
                    for sl in range(NH):
                        st = st0 + sl
                        xb = XBh[nx % 2]
                        xo = 0
                        nx += 1

                        def f():
                            nc.tensor.matmul(xb.t[:, xo:xo + 128], lhsT=lhsB.t[:, st, 0, :], rhs=u_.t[:, st // 4, :], start=True, stop=True)
                            return nc.tensor.matmul(xb.t[:, xo + 128:xo + 256], lhsT=lhsB.t[:, st, 1, :], rhs=u_.t[:, st // 4, :], start=True, stop=True)
                        kb.op(PE, f, r=[lhsB, u_], w=[xb])
                        xr, xi_ = xb.t[:, xo:xo + 128], xb.t[:, xo + 128:xo + 256]
                        c_, s_ = cosT.t[:, st, :], sinT.t[:, st, :]
                        kb.op(DVE, lambda: TT(vr.t[:, sl, :], xr, c_, ALU.mult), r=[xb, cosT], p=[vr])
                        kb.op(DVE, lambda: TT(ta.t[:], xi_, s_, ALU.mult), r=[xb, sinT], w=[ta])
                        kb.op(DVE, lambda: TT(vi.t[:, sl, :], xi_, c_, ALU.mult), r=[xb, cosT], p=[vi])
                        kb.op(DVE, lambda: TT(tb.t[:], xr, s_, ALU.mult), r=[xb, sinT], w=[tb])
                        kb.op(DVE, lambda: TT(vr.t[:, sl, :], vr.t[:, sl, :], ta.t[:], ALU.add), r=[ta], p=[vr])
                        kb.op(DVE, lambda: TT(vi.t[:, sl, :], vi.t[:, sl, :], tb.t[:], ALU.subtract), r=[tb], p=[vi])
                        yield
                    while fifo:
                        fifo.pop(0)()
                    for sl in range(NH):
                        st = st0 + sl
                        for comp, (src, dst) in enumerate(((vr, gr), (vi, gi))):
                            kb.op(DVE, lambda: nc.vector.tensor_tensor_scan(out=dst.t[:, sl, :], data0=rho.t[:, st:st + 1].to_broadcast([128, 128]), data1=src.t[:, sl, :], initial=icur.t[:, comp, st:st + 1], op0=ALU.mult, op1=ALU.add), r=[src, rho, icur], p=[dst])
                        if sl % 2 == 1:
                            yield
                    gre, gie = gr.t[:, :, 127], gi.t[:, :, 127]
                    c8, s8 = e128.t[:, 0, st0:st0 + NH], e128.t[:, 1, st0:st0 + NH]
                    kb.op(DVE, lambda: TT(it4.t[:, 0, :], gre, c8, ALU.mult), r=[gr, e128], p=[it4])
                    kb.op(DVE, lambda: TT(it4.t[:, 1, :], gie, s8, ALU.mult), r=[gi, e128], p=[it4])
                    kb.op(DVE, lambda: TT(it4.t[:, 2, :], gre, s8, ALU.mult), r=[gr, e128], p=[it4])
                    kb.op(DVE, lambda: TT(it4.t[:, 3, :], gie, c8, ALU.mult), r=[gi, e128], p=[it4])
                    kb.op(DVE, lambda: TT(inxt.t[:, 0, st0:st0 + NH], it4.t[:, 0, :], it4.t[:, 1, :], ALU.subtract), r=[it4], p=[inxt])
                    kb.op(DVE, lambda: TT(inxt.t[:, 1, st0:st0 + NH], it4.t[:, 2, :], it4.t[:, 3, :], ALU.add), r=[it4], p=[inxt])
                    if qtile:
                        hr_, hi_ = hbr[nq % 2], hbi[nq % 2]
                        nq += 1
                        for hh in range(NH // 4):
                            sl4 = slice(hh * 4, hh * 4 + 4)
                            st4 = slice(st0 + hh * 4, st0 + hh * 4 + 4)
                            q4 = lambda b_: b_.t[:, sl4, :].rearrange("p a b -> p (a b)")
                            t4 = lambda b_: b_.t[:, st4, :].rearrange("p a b -> p (a b)")
                            kb.op(POOL, lambda: GT(pa.t[:], q4(gr), t4(cosT), ALU.mult), r=[gr, cosT], w=[pa])
                            kb.op(POOL, lambda: GT(pb2.t[:], q4(gi), t4(sinT), ALU.mult), r=[gi, sinT], w=[pb2])
                            kb.op(POOL, lambda: GT(q4(hr_), pa.t[:], pb2.t[:], ALU.subtract), r=[pa, pb2], p=[hr_])
                            kb.op(POOL, lambda: GT(pa.t[:], q4(gr), t4(sinT), ALU.mult), r=[gr, sinT], w=[pa])
                            kb.op(POOL, lambda: GT(pb2.t[:], q4(gi), t4(cosT), ALU.mult), r=[gi, cosT], w=[pb2])
                            kb.op(POOL, lambda: GT(q4(hi_), pa.t[:], pb2.t[:], ALU.add), r=[pa, pb2], p=[hi_])

                        def cstage(st0=st0, hr_=hr_, hi_=hi_, u_=u_, yst_=yst_):
                            for kcl in range(NH // 4):
                                kc = st0 // 4 + kcl

                                def f():
                                    inst = None
                                    for s4 in range(4):
                                        sl_ = kcl * 4 + s4
                                        st_ = st0 + sl_
                                        nc.tensor.matmul(YB.t[:, 256:384], lhsT=lhsC.t[:, st_, 0, :], rhs=hr_.t[:, sl_, :], start=(s4 == 0), stop=False)
                                        inst = nc.tensor.matmul(YB.t[:, 256:384], lhsT=lhsC.t[:, st_, 1, :], rhs=hi_.t[:, sl_, :], start=False, stop=(s4 == 3))
                                    return inst
                                kb.op(PE, f, r=[lhsC, hr_, hi_], w=[YB])
                                kb.op(DVE, lambda: nc.vector.scalar_tensor_tensor(out=yst_.t[:, kc, :], in0=u_.t[:, kc, :], scalar=dsk.t[:, kc:kc + 1], in1=YB.t[:, 256:384], op0=ALU.mult, op1=ALU.add), r=[u_, dsk, YB], p=[yst_])
                        fifo.append(cstage)
                        if qr == NQ - 1:
                            fifo.append(lambda t0=t0, yst_=yst_: kb.dma(POOL, y_s.t[:, :, t0:t0 + 128].rearrange("k p t -> p k t"), yst_.t[:], sb=yst_, r=[yst_], p=[y_s]))
                    yield
            while fifo:
                fifo.pop(0)()
                yield

        n_slices = (T // 128) * NQ * (NH + NH // 2 + 1) + 4
        n_its = H * sum(2 * (t0 // 128 + nt // 128) for (t0, nt) in qchunks)
        rate = RATE_SCALE * n_slices / n_its
        sg = ssm_gen()
        pull = dict(acc=0.0)
        cnt = dict(s=0, pt=0, rl=0, av=0, tp=0, ch=0, g=0)
        for h in range(H):
            sl = h % 2
            kt_, vt_, qt_ = kTt[sl], Vt[0], qTt[sl]
            for (src_, dst_, rows_) in ((w_gate_d, w_gate_b, D), (w_up_d, w_up_b, D), (w_down_d, w_down_b, F)):
                ra, rb_ = (rows_ * h) // H, (rows_ * (h + 1)) // H
                kb.dma(POOL, dst_.t[ra:rb_, :], src_.t[ra:rb_, :], sb=dst_, p=[dst_])
            for m in range(2):
                kb.dma(SP, kt_.t[:, m, :], kT_s.t[2 * h + m, :, :], sb=kt_, r=[kT_s], p=[kt_])
                kb.dma(SP, qt_.t[:, m, :], qT_s.t[2 * h + m, :, QR0:T], sb=qt_, r=[qT_s], p=[qt_])
            kb.dma(SP, vt_.t[:, :, 0:256], vv_s.t[:, h * 256:(h + 1) * 256].rearrange("(k p) e -> p k e", p=128), sb=vt_, r=[vv_s], p=[vt_])
            its = []
            for (t0, nt) in qchunks:
                for m in range(2):
                    for kt in range(t0 // 128 + nt // 128):
                        its.append((t0, nt, m, kt))

            def emit_S(it):
                t0, nt, m, kt = it
                qt0 = t0 // 128
                qo = t0 - QR0
                j0 = max(0, kt - qt0)
                S = FB[4]
                diag = kt >= qt0

                def fs():
                    inst = nc.tensor.matmul(S.t[:, j0 * 128:nt], lhsT=kt_.t[:, m, kt * 128:(kt + 1) * 128], rhs=qt_.t[:, m, qo + j0 * 128:qo + nt], start=True, stop=not diag)
                    if diag:
                        inst = nc.tensor.matmul(S.t[:, j0 * 128:(j0 + 1) * 128], lhsT=negm.t[:], rhs=ident.t[:], start=False, stop=True)
                    return inst
                kb.op(PE, fs, r=[kt_, qt_, negm, ident], w=[S])
                pb = pt[cnt["pt"] % 3]
                cnt["pt"] += 1
                kb.op(ACT, lambda: nc.scalar.activation(out=pb.t[:, j0 * 128:nt], in_=S.t[:, j0 * 128:nt], func=AF.Exp), r=[S], w=[pb])
                return pb

            def emit_PV(it, pb):
                t0, nt, m, kt = it
                ns = nt // 128
                qt0 = t0 // 128
                j0 = max(0, kt - qt0)
                O = FB[0:ns]

                def f():
                    inst = None
                    for j in range(j0, ns):
                        inst = nc.tensor.matmul(O[j].t[:, 0:257], lhsT=pb.t[:, j * 128:(j + 1) * 128], rhs=vt_.t[:, kt, :], start=(kt == 0), stop=(kt == qt0 + j))
                    return inst
                if kt == 0:
                    kb.op(PE, f, r=[pb, vt_], w=O[j0:ns])
                else:
                    kb.op(PE, f, r=[pb, vt_], p=O[j0:ns])
                if kt != qt0 + ns - 1:
                    return
                ob = osb[cnt["g"] % 2]
                cnt["g"] += 1
                for j in range(ns):
                    kb.op(ACT, lambda j=j: nc.scalar.copy(out=ob.t[:, j, :], in_=O[j].t[:, 0:257]), r=[O[j]], p=[ob])
                if m == 0:
                    cnt["ch"] += 1
                ast = aTst[cnt["ch"] % 2]
                for j in range(ns):
                    rlb = rl[cnt["rl"] % 4]
                    cnt["rl"] += 1
                    kb.op(DVE, lambda: nc.vector.tensor_scalar(out=rlb.t[:, 0:1], in0=ob.t[:, j, 256:257], scalar1=1e-30, scalar2=None, op0=ALU.add), r=[ob], w=[rlb])
                    kb.op(DVE, lambda: nc.vector.reciprocal(out=rlb.t[:, 0:1], in_=rlb.t[:, 0:1]), w=[rlb])
                    if m == 0:
                        kb.op(ACT, lambda: nc.scalar.activation(out=a1.t[:, j, :], in_=ob.t[:, j, 0:256], func=AF.Copy, scale=rlb.t[:, 0:1]), r=[ob, rlb], p=[a1])
                    else:
                        avb = av[cnt["av"] % 2]
                        abb = ab[cnt["av"] % 2]
                        cnt["av"] += 1
                        kb.op(DVE, lambda: nc.vector.tensor_scalar(out=rlb.t[:, 0:1], in0=rlb.t[:, 0:1], scalar1=lamneg.t[:, 0:1], scalar2=None, op0=ALU.mult), r=[lamneg], w=[rlb])
                        kb.op(DVE, lambda: nc.vector.scalar_tensor_tensor(out=avb.t[:], in0=ob.t[:, j, 0:256], scalar=rlb.t[:, 0:1], in1=a1.t[:, j, :], op0=ALU.mult, op1=ALU.add), r=[ob, rlb, a1], w=[avb])
                        kb.op(ACT, lambda: nc.scalar.activation(out=aj.t[:], in_=avb.t[:], func=AF.Square, accum_out=rlb.t[:, 1:2]), r=[avb], w=[aj, rlb])
                        rstd_from_ss(rlb.t[:, 1:2], rlb, rlb.t[:, 1:2], 256, [rlb])
                        kb.op(DVE, lambda: nc.vector.scalar_tensor_tensor(out=abb.t[:], in0=avb.t[:], scalar=rlb.t[:, 1:2], in1=subg.t[:], op0=ALU.mult, op1=ALU.mult), r=[avb, rlb, subg], w=[abb])
                        cnt["tp"] = transposes(abb, lambda k: abb.t[:, k * 128:(k + 1) * 128], 2, ast,
                                               lambda k0, c_: ast.t[:, k0:k0 + c_, j * 128:(j + 1) * 128], cnt["tp"])
                if m == 1:
                    kb.dma(POOL, mixT_s.t[2 * h:2 * h + 2, :, t0:t0 + nt].rearrange("k p t -> p k t"), ast.t[:, :, 0:nt], sb=ast, r=[ast], p=[mixT_s])

            cur = emit_S(its[0])
            for i, it in enumerate(its):
                nxt = emit_S(its[i + 1]) if i + 1 < len(its) else None
                emit_PV(it, cur)
                cur = nxt
                pull["acc"] += rate
                while pull["acc"] >= 1.0:
                    pull["acc"] -= 1.0
                    next(sg, None)
        for _ in sg:
            pass
        kb.barrier(ph + allbufs + FB + BB)
        kb.retire(ph)

    with ExitStack() as es:
        def sb(name, shape, dt):
            return Buf(name, es.enter_context(nc.sbuf_tensor(kb.name(name), list(shape), dt)))
        TT = lambda o, a, b, op: nc.vector.tensor_tensor(out=o, in0=a, in1=b, op=op)
        wglu = sb("wglu", [128, NKC, SW], BF16)
        bglu = sb("bglu", [128, NKC], F32)
        outg = sb("outg", [128, NKC], F32)
        yv = sb("yv", [128, NKC, 512], F32)
        yt1 = sb("yt1", [128, NKC, 512], F32)
        ygf = sb("ygf", [128, NKC, 512], F32)
        ygb = sb("ygb", [128, NKC, 512], BF16)
        sgz = sb("sgz", [128, NKC, 512], F32)
        yo = sb("yo", [128, NKC, 512], F32)
        ysq = sb("ysq", [128, NKC, 512], BF16)
        rr = sb("rr", [128, 512], F32)
        so = [sb("so", [128, NKC, 512], BF16) for _ in range(2)]
        ph = [wglu, bglu, outg, yv, yt1, ygf, ygb, sgz, yo, ysq, rr] + so
        load(bglu, bglu_d)
        load(outg, outg_d)
        kb.dma(SP, wglu.t[:], w_glu_b.t[:, :].rearrange("(k p) c -> p k c", p=128), sb=wglu, r=[w_glu_b], w=[wglu])
        for ci, (t0, nt) in enumerate(qchunks):
            V = lambda b_: b_.t[:, :, 0:nt]
            kb.dma(SP, V(yv), y_s.t[:, :, t0:t0 + nt].rearrange("k p t -> p k t"), sb=yv, r=[y_s], w=[yv])
            kb.op(DVE, lambda: TT(V(yt1), V(yv), V(yv), ALU.mult), r=[yv], w=[yt1])
            kb.op(DVE, lambda: nc.vector.tensor_scalar(out=V(yt1), in0=V(yt1), scalar1=0.044715, scalar2=1.0, op0=ALU.mult, op1=ALU.add), w=[yt1])
            kb.op(DVE, lambda: TT(V(yt1), V(yt1), V(yv), ALU.mult), r=[yv], w=[yt1])
            kb.op(ACT, lambda: nc.scalar.activation(out=V(yt1), in_=V(yt1), func=AF.Sigmoid, scale=1.5957691216), w=[yt1])
            kb.op(DVE, lambda: TT(V(ygf), V(yt1), V(yv), ALU.mult), r=[yt1, yv], w=[ygf])
            kb.op(ACT, lambda: nc.scalar.copy(out=V(ygb), in_=V(ygf)), r=[ygf], w=[ygb])
            for jc in range(NKC):
                zb = FB[jc % 4]

                def f():
                    inst = None
                    for kc in range(NKC):
                        inst = nc.tensor.matmul(zb.t[:, 0:nt], lhsT=wglu.t[:, kc, jc * 128:(jc + 1) * 128], rhs=ygb.t[:, kc, 0:nt], start=(kc == 0), stop=(kc == NKC - 1))
                    return inst
                kb.op(PE, f, r=[wglu, ygb], w=[zb])
                kb.op(ACT, lambda: nc.scalar.activation(out=sgz.t[:, jc, 0:nt], in_=zb.t[:, 0:nt], func=AF.Sigmoid, bias=bglu.t[:, jc:jc + 1]), r=[zb, bglu], p=[sgz])
            kb.op(DVE, lambda: TT(V(yo), V(ygf), V(sgz), ALU.mult), r=[ygf, sgz], w=[yo])
            kb.op(ACT, lambda: nc.scalar.activation(out=V(ysq), in_=V(yo), func=AF.Square), r=[yo], w=[ysq])
            sb_ = FB[4]

            def f():
                inst = None
                for kc in range(NKC):
                    inst = nc.tensor.matmul(sb_.t[:, 0:nt], lhsT=onesb.t[:], rhs=ysq.t[:, kc, 0:nt], start=(kc == 0), stop=(kc == NKC - 1))
                return inst
            kb.op(PE, f, r=[onesb, ysq], w=[sb_])
            rstd_from_ss(sb_.t[:, 0:nt], rr, rr.t[:, 0:nt], SW, [sb_])
            so_ = so[ci % 2]
            for kc in range(NKC):
                kb.op(DVE, lambda: nc.vector.scalar_tensor_tensor(out=so_.t[:, kc, 0:nt], in0=yo.t[:, kc, 0:nt], scalar=outg.t[:, kc:kc + 1], in1=rr.t[:, 0:nt], op0=ALU.mult, op1=ALU.mult), r=[yo, outg, rr], p=[so_])
            kb.dma(POOL, mixT_s.t[AW // 128:KD, :, t0:t0 + nt].rearrange("k p t -> p k t"), so_.t[:, :, 0:nt], sb=so_, r=[so_], p=[mixT_s])
        kb.barrier(ph + allbufs + FB + BB)
        kb.retire(ph)

    with ExitStack() as es:
        def sb(name, shape, dt):
            return Buf(name, es.enter_context(nc.sbuf_tensor(kb.name(name), list(shape), dt)))
        fnorm = sb("fnorm", [128, D], F32)
        mixc = sb("mixc", [128, KD, 512], BF16)
        wt = [sb("wt", [128, KD, 256], BF16) for _ in range(2)]
        xm = sb("xm", [128, 4, D], F32)
        xp = [sb("xp", [128, 256], F32) for _ in range(4)]
        junk = sb("junk", [128, D], BF16)
        hb = sb("hb", [128, D], BF16)
        h2t = [sb("h2t", [128, KD, 128], BF16) for _ in range(2)]
        ssq = [sb("ssq", [128, 1], F32) for _ in range(2)]
        ph = [fnorm, junk, mixc, xm, hb] + wt + xp + h2t + ssq
        load(fnorm, fnorm_d)
        n_w = n_acc = n_tp = n_xp = n_t = 0
        for (t0, nt) in qchunks:
            ns = nt // 128
            kb.dma(SP, mixc.t[:, :, 0:nt], mixT_s.t[:, :, t0:t0 + nt].rearrange("k p t -> p k t"), sb=mixc, r=[mixT_s], w=[mixc])
            for cb in range(D // 256):
                wtile = wt[n_w % 2]
                n_w += 1
                kb.dma(SP, wtile.t[:], w_out_b.t[:, cb * 256:(cb + 1) * 256].rearrange("(k p) c -> p k c", p=128), sb=wtile, r=[w_out_b], w=[wtile])
                accs = [FB[(n_acc + j) % 6] for j in range(ns)]
                n_acc += ns

                def f(accs=accs, wtile=wtile, ns=ns):
                    inst = None
                    for k in range(KD):
                        for j in range(ns):
                            inst = nc.tensor.matmul(accs[j].t[:, 0:256], lhsT=mixc.t[:, k, j * 128:(j + 1) * 128], rhs=wtile.t[:, k, :], start=(k == 0), stop=(k == KD - 1))
                    return inst
                kb.op(PE, f, r=[mixc, wtile], w=accs)
                for j in range(ns):
                    xp_ = xp[n_xp % 4]
                    n_xp += 1
                    r0 = t0 + j * 128
                    kb.dma(SP, xp_.t[:], x_ext.t[r0:r0 + 128, cb * 256:(cb + 1) * 256], sb=xp_, w=[xp_])
                    kb.op(DVE, lambda j=j, xp_=xp_, accs=accs, cb=cb: nc.vector.tensor_tensor(out=xm.t[:, j, cb * 256:(cb + 1) * 256], in0=accs[j].t[:, 0:256], in1=xp_.t[:], op=ALU.add), r=[accs[j], xp_], p=[xm])
            for j in range(ns):
                r0 = t0 + j * 128
                h2_, sq_ = h2t[n_t % 2], ssq[n_t % 2]
                n_t += 1
                kb.dma(POOL, xmid_s.t[r0:r0 + 128, :], xm.t[:, j, :], sb=xm, r=[xm], p=[xmid_s])
                kb.op(ACT, lambda j=j, sq_=sq_: nc.scalar.activation(out=junk.t[:], in_=xm.t[:, j, :], func=AF.Square, accum_out=sq_.t[:, 0:1]), r=[xm], w=[junk, sq_])
                rstd_from_ss(sq_.t[:, 0:1], sq_, sq_.t[:, 0:1], D, [sq_])
                kb.op(DVE, lambda j=j, sq_=sq_: nc.vector.scalar_tensor_tensor(out=hb.t[:], in0=xm.t[:, j, :], scalar=sq_.t[:, 0:1], in1=fnorm.t[:], op0=ALU.mult, op1=ALU.mult), r=[xm, sq_, fnorm], w=[hb])
                n_tp = transposes(hb, lambda k: hb.t[:, k * 128:(k + 1) * 128], KD, h2_,
                                  lambda k0, c_, h2_=h2_: h2_.t[:, k0:k0 + c_, :], n_tp)
                kb.dma(POOL, h2T_s.t[:, :, r0:r0 + 128].rearrange("k p t -> p k t"), h2_.t[:], sb=h2_, r=[h2_], p=[h2T_s])
        kb.barrier(ph + allbufs + FB + BB)
        kb.retire(ph)

    with ExitStack() as es:
        def sb(name, shape, dt):
            return Buf(name, es.enter_context(nc.sbuf_tensor(kb.name(name), list(shape), dt)))
        h2c = sb("h2c", [128, KD, 512], BF16)
        wg = [sb("wg", [128, KD, 512], BF16) for _ in range(2)]
        wu = [sb("wu", [128, KD, 512], BF16) for _ in range(2)]
        gh = sb("gh", [128, NF, 2], F32)
        cw = sb("cw", [128, 3, NF], F32)
        cbs = sb("cbs", [128, NF], F32)
        gs = [sb("gs", [128, 514], F32) for _ in range(2)]
        tm = [sb("tm", [128, 512], F32) for _ in range(2)]
        tm2 = [sb("tm2", [128, 512], F32) for _ in range(2)]
        ast = [sb("ast", [128, 512], BF16) for _ in range(3)]
        ph = [h2c, gh, cw, cbs] + wg + wu + gs + tm + tm2 + ast
        load(cw, convw_d)
        load(cbs, convb_d)
        n_w = n_f = 0
        for ci, (t0, nt) in enumerate(qchunks):
            halo = (ci == 0)
            kb.dma(SP, h2c.t[:, :, 0:nt], h2T_s.t[:, :, t0:t0 + nt].rearrange("k p t -> p k t"), sb=h2c, r=[h2T_s], w=[h2c])
            c0 = 0
            while c0 < F:
                ncols = min(512, F - c0)
                wg_, wu_ = wg[n_w % 2], wu[n_w % 2]
                n_w += 1
                kb.dma(SP, wg_.t[:, :, 0:ncols], w_gate_b.t[:, c0:c0 + ncols].rearrange("(k p) c -> p k c", p=128), sb=wg_, r=[w_gate_b], w=[wg_])
                if not halo:
                    kb.dma(SP, wu_.t[:, :, 0:ncols], w_up_b.t[:, c0:c0 + ncols].rearrange("(k p) c -> p k c", p=128), sb=wu_, r=[w_up_b], w=[wu_])
                for jb in range(ncols // 128):
                    fi = c0 // 128 + jb
                    Gb, Ub = FB[(2 * n_f) % 6], FB[(2 * n_f + 1) % 6]
                    gs_, tm_, tm2_, ast_ = gs[n_f % 2], tm[n_f % 2], tm2[n_f % 2], ast[n_f % 3]
                    n_f += 1

                    def fg(Gb=Gb, jb=jb, wg_=wg_, nt=nt):
                        inst = None
                        for k in range(KD):
                            inst = nc.tensor.matmul(Gb.t[:, 0:nt], lhsT=wg_.t[:, k, jb * 128:(jb + 1) * 128], rhs=h2c.t[:, k, 0:nt], start=(k == 0), stop=(k == KD - 1))
                        return inst
                    kb.op(PE, fg, r=[wg_, h2c], w=[Gb])
                    if halo:
                        kb.op(ACT, lambda Gb=Gb, fi=fi, nt=nt: nc.scalar.copy(out=gh.t[:, fi, :], in_=Gb.t[:, nt - 2:nt]), r=[Gb], p=[gh])
                        continue

                    def fu(Ub=Ub, jb=jb, wu_=wu_, nt=nt):
                        inst = None
                        for k in range(KD):
                            inst = nc.tensor.matmul(Ub.t[:, 0:nt], lhsT=wu_.t[:, k, jb * 128:(jb + 1) * 128], rhs=h2c.t[:, k, 0:nt], start=(k == 0), stop=(k == KD - 1))
                        return inst
                    kb.op(PE, fu, r=[wu_, h2c], w=[Ub])
                    kb.op(DVE, lambda gs_=gs_, fi=fi: nc.vector.tensor_copy(out=gs_.t[:, 0:2], in_=gh.t[:, fi, :]), r=[gh], w=[gs_])
                    kb.op(ACT, lambda gs_=gs_, Gb=Gb, nt=nt: nc.scalar.copy(out=gs_.t[:, 2:2 + nt], in_=Gb.t[:, 0:nt]), r=[Gb], p=[gs_])
                    kb.op(DVE, lambda gs_=gs_, fi=fi, nt=nt: nc.vector.tensor_copy(out=gh.t[:, fi, :], in_=gs_.t[:, nt:nt + 2]), r=[gs_], w=[gh])
                    kb.op(ACT, lambda tm_=tm_, Gb=Gb, fi=fi, nt=nt: nc.scalar.activation(out=tm_.t[:, 0:nt], in_=Gb.t[:, 0:nt], func=AF.Identity, scale=cw.t[:, 2, fi:fi + 1], bias=cbs.t[:, fi:fi + 1]), r=[Gb, cw, cbs], w=[tm_])
                    kb.op(DVE, lambda tm_=tm_, gs_=gs_, fi=fi, nt=nt: nc.vector.scalar_tensor_tensor(out=tm_.t[:, 0:nt], in0=gs_.t[:, 1:1 + nt], scalar=cw.t[:, 1, fi:fi + 1], in1=tm_.t[:, 0:nt], op0=ALU.mult, op1=ALU.add), r=[gs_, cw], w=[tm_])
                    kb.op(DVE, lambda tm_=tm_, gs_=gs_, fi=fi, nt=nt: nc.vector.scalar_tensor_tensor(out=tm_.t[:, 0:nt], in0=gs_.t[:, 0:nt], scalar=cw.t[:, 0, fi:fi + 1], in1=tm_.t[:, 0:nt], op0=ALU.mult, op1=ALU.add), r=[gs_, cw], w=[tm_])
                    kb.op(ACT, lambda tm_=tm_, tm2_=tm2_, nt=nt: nc.scalar.activation(out=tm2_.t[:, 0:nt], in_=tm_.t[:, 0:nt], func=AF.Silu), r=[tm_], w=[tm2_])
                    kb.op(DVE, lambda tm2_=tm2_, Ub=Ub, ast_=ast_, nt=nt: nc.vector.tensor_tensor(out=ast_.t[:, 0:nt], in0=tm2_.t[:, 0:nt], in1=Ub.t[:, 0:nt], op=ALU.mult), r=[tm2_, Ub], w=[ast_])
                    kb.dma(POOL, actT_s.t[fi, :, t0:t0 + nt], ast_.t[:, 0:nt], sb=ast_, r=[ast_], p=[actT_s])
                c0 += ncols
        kb.barrier(ph + allbufs + FB + BB)
        kb.retire(ph)

    with ExitStack() as es:
        def sb(name, shape, dt):
            return Buf(name, es.enter_context(nc.sbuf_tensor(kb.name(name), list(shape), dt)))
        actc = sb("actc", [128, NF, 512], BF16)
        wd = [sb("wd", [128, 8, 512], BF16) for _ in range(4)]
        xmp = [sb("xmp", [128, 512], F32) for _ in range(4)]
        ost = [sb("ost", [128, 512], F32) for _ in range(4)]
        ph = [actc] + wd + xmp + ost
        n_w = n_e = 0
        for (t0, nt) in qchunks[1:]:
            f0 = 0
            while f0 < NF:
                f1 = min(NF, f0 + 16)
                kb.dma(SP, actc.t[:, f0:f1, :], actT_s.t[f0:f1, :, t0:t0 + 512].rearrange("k p t -> p k t"), sb=actc, r=[actT_s], w=[actc] if f0 == 0 else (), p=() if f0 == 0 else [actc])
                f0 = f1
            for cb in range(D // 512):
                accs = FB[0:4]
                for kg in range((NF + 7) // 8):
                    nk = min(8, NF - kg * 8)
                    wd_ = wd[n_w % 4]
                    n_w += 1
                    kb.dma(SP, wd_.t[:, 0:nk, :], w_down_b.t[kg * 1024:kg * 1024 + nk * 128, cb * 512:(cb + 1) * 512].rearrange("(k p) c -> p k c", p=128), sb=wd_, r=[w_down_b], w=[wd_])

                    def f(kg=kg, nk=nk, wd_=wd_, accs=accs):
                        inst = None
                        for kk in range(nk):
                            fi = kg * 8 + kk
                            for sub in range(4):
                                inst = nc.tensor.matmul(accs[sub].t[:], lhsT=actc.t[:, fi, sub * 128:(sub + 1) * 128], rhs=wd_.t[:, kk, :], start=(fi == 0), stop=(fi == NF - 1))
                        return inst
                    if kg == 0:
                        kb.op(PE, f, r=[actc, wd_], w=accs)
                    else:
                        kb.op(PE, f, r=[actc, wd_], p=accs)
                for sub in range(4):
                    xp, os_ = xmp[n_e % 4], ost[n_e % 4]
                    n_e += 1
                    r0 = t0 + sub * 128
                    kb.dma(SP, xp.t[:], xmid_s.t[r0:r0 + 128, cb * 512:(cb + 1) * 512], sb=xp, r=[xmid_s], w=[xp])
                    kb.op(DVE, lambda sub=sub, xp=xp, os_=os_, accs=accs: nc.vector.tensor_tensor(out=os_.t[:], in0=accs[sub].t[:], in1=xp.t[:], op=ALU.add), r=[accs[sub], xp], w=[os_])
                    kb.dma(POOL, y_out.t[r0 - HALF:r0 - HALF + 128, cb * 512:(cb + 1) * 512], os_.t[:], sb=os_, r=[os_], p=[y_out])
        kb.barrier(ph + allbufs + FB + BB)
        kb.retire(ph)
    return nc


def prep_shared(cfg, inp):
    c = derive(cfg)
    D, H, G, F = c["D"], c["H"], c["G"], c["F"]
    SW, NST, NKC, NF = c["SW"], c["NST"], c["NKC"], c["NF"]
    f32 = np.float32
    rep = lambda v, n=128: np.ascontiguousarray(np.broadcast_to(np.asarray(v, f32).reshape(1, -1), (n, np.asarray(v).size)))
    d = {}
    d["anorm"] = rep(inp["attn_norm"][0])
    d["fnorm"] = rep(inp["ffn_norm"][0])
    d["w_in"] = np.ascontiguousarray(inp["w_in"][0], f32)
    d["w_out"] = np.ascontiguousarray(inp["w_out"][0], f32)
    d["w_gate"] = np.ascontiguousarray(inp["w_gate"][0], f32)
    d["w_up"] = np.ascontiguousarray(inp["w_up"][0], f32)
    d["w_down"] = np.ascontiguousarray(inp["w_down"][0], f32)
    d["w_glu"] = np.ascontiguousarray(inp["ssm_w_glu"][0], f32)
    d["qg"] = np.ascontiguousarray(np.asarray(inp["q_gain"][0], f32).reshape(128, 1))
    d["kg"] = np.ascontiguousarray(np.asarray(inp["k_gain"][0], f32).reshape(128, 1))
    lam = np.stack([inp["lam_q1"][0], inp["lam_k1"][0], inp["lam_q2"][0], inp["lam_k2"][0]]).astype(f32)
    d["lamv"] = np.ascontiguousarray(np.broadcast_to(lam[None], (128, 4, 128)))
    d["subg"] = rep(inp["sub_gain"][0])
    a_re = np.asarray(inp["ssm_a_re"][0], f32)
    a_im = np.asarray(inp["ssm_a_im"][0], f32)
    ldt = np.asarray(inp["ssm_log_dt"][0], f32)
    st_l = lambda a: np.ascontiguousarray(a.reshape(NST, 2, 64).transpose(1, 2, 0).reshape(128, NST))
    d["sAre"] = st_l(a_re)
    d["sAim"] = st_l(a_im)
    d["sLdt"] = st_l(np.broadcast_to(ldt[:, None], (G, 64)))
    ch_l = lambda a: np.ascontiguousarray(np.broadcast_to(a.reshape(NKC, 8, 1, 64), (NKC, 8, 16, 64)).transpose(1, 2, 0, 3).reshape(128, NKC * 64))
    d["cAre"] = ch_l(a_re)
    d["cAim"] = ch_l(a_im)
    d["cLdt"] = ch_l(np.broadcast_to(ldt[:, None], (G, 64)))
    bt = lambda b: np.ascontiguousarray(np.asarray(b, f32).reshape(NKC, 8, 64, 16).transpose(1, 3, 0, 2).reshape(128, NKC * 64))
    d["BTre"] = bt(inp["ssm_b_re"][0])
    d["BTim"] = bt(inp["ssm_b_im"][0])
    ct = lambda cc: np.ascontiguousarray(np.asarray(cc, f32).reshape(NST, 2, 16, 64).transpose(1, 3, 0, 2).reshape(128, NST, 16))
    d["CTre"] = ct(inp["ssm_c_re"][0])
    d["CTim"] = ct(inp["ssm_c_im"][0])
    pk = lambda v: np.ascontiguousarray(np.asarray(v, f32).reshape(NKC, 128).T)
    d["dsk"] = pk(inp["ssm_d"][0])
    d["bglu"] = pk(inp["ssm_b_glu"][0])
    d["outg"] = pk(inp["ssm_out_gain"][0])
    d["convw"] = np.ascontiguousarray(np.asarray(inp["conv_w"][0], f32).reshape(3, NF, 128).transpose(2, 0, 1))
    d["convb"] = np.ascontiguousarray(np.asarray(inp["conv_b"][0], f32).reshape(NF, 128).T)
    bf = ml_dtypes.bfloat16
    d["ident"] = np.eye(128, dtype=f32).astype(bf)
    kk = np.arange(128)
    d["cmask"] = (kk[:, None] <= kk[None, :]).astype(f32).astype(bf)
    d["onesb"] = np.ones((128, 128), f32).astype(bf)
    d["negm"] = (-30000.0 * (kk[None, :] > kk[:, None])).astype(f32).astype(bf)
    d["iota_t"] = np.ascontiguousarray(np.broadcast_to(np.arange(128, dtype=f32)[None], (128, 128)))
    rm = np.zeros((128, 8), f32)
    for j in range(8):
        rm[j * 16:(j + 1) * 16, j] = 1.0
    d["rowmask"] = rm
    return d


def run(cfg, inp, trace=False):
    c = derive(cfg)
    HALF, D = c["HALF"], c["D"]
    x = np.asarray(inp["x"], np.float32)
    B = x.shape[0]
    assert x.shape[1] == 2 * HALF
    shared = prep_shared(cfg, inp)
    in_maps = []
    for core in range(2 * B):
        b, r = core // 2, core % 2
        m = dict(shared)
        xe = np.zeros((2 * HALF, D), np.float32)
        if r == 1:
            xe[:] = x[b]
        else:
            xe[HALF:] = x[b, :HALF]
        m["x_ext"] = xe
        m["flag"] = np.full((128, 1), float(r), np.float32)
        in_maps.append(m)
    nc = build_nc(cfg)
    res = run_bass_kernel_spmd(nc, in_maps, core_ids=list(range(2 * B)), **({"trace": True} if trace else {}))
    out = np.empty((B, 2 * HALF, D), np.float32)
    for core in range(2 * B):
        b, r = core // 2, core % 2
        out[b, r * HALF:(r + 1) * HALF] = res.results[core]["y"]
    return out, res


def kernel(**inputs):
    out, _ = run(CFG_FULL, inputs)
    return out
```

```python
import math
import numpy as np
import ml_dtypes
from contextlib import ExitStack
import concourse.bass as bass
import concourse.mybir as mybir
from concourse.bass_utils import run_bass_kernel_spmd

F32, BF16, I32 = mybir.dt.float32, mybir.dt.bfloat16, mybir.dt.int32
AF = mybir.ActivationFunctionType
ALU = mybir.AluOpType
EPS = 1e-6
PI = math.pi

RATE_SCALE = 1.08
CFG_FULL = dict(D=4096, H=12, G=64, F=11008, HALF=2048)


def derive(cfg):
    c = dict(cfg)
    c["AW"] = c["H"] * 256
    c["SW"] = c["G"] * 16
    assert c["AW"] + c["SW"] == c["D"]
    c["KD"] = c["D"] // 128
    c["INW"] = 3 * c["AW"] + c["SW"]
    c["NST"] = c["G"] // 2
    c["NKC"] = c["SW"] // 128
    c["NF"] = c["F"] // 128
    c["T"] = 2 * c["HALF"]
    c["QR0"] = c["HALF"] - 128
    c["qchunks"] = [(c["HALF"] - 128, 128)] + [(c["HALF"] + 512 * i, 512) for i in range(c["HALF"] // 512)]
    return c


class Eng:
    def __init__(self, nc, name, e):
        self.e = e
        self.name = name
        self.sem = nc.semaphore("es_" + name).__enter__()
        self.cnt = 0
        self.seen = {}


class Buf:
    def __init__(self, name, t):
        self.name = name
        self.t = t
        self.w = {}
        self.r = {}
        self.sem = None
        self.val = 0

    def __getitem__(self, idx):
        return self.t[idx]


def _upd(d, tok):
    k = id(tok[0])
    if k not in d or d[k][1] < tok[1]:
        d[k] = tok


class KB:
    def __init__(self, nc):
        self.nc = nc
        self.PE = Eng(nc, "pe", nc.tensor)
        self.ACT = Eng(nc, "act", nc.scalar)
        self.DVE = Eng(nc, "dve", nc.vector)
        self.POOL = Eng(nc, "pool", nc.gpsimd)
        self.SP = Eng(nc, "sp", nc.sync)
        self.engs = [self.PE, self.ACT, self.DVE, self.POOL, self.SP]
        self.nsem = 0
        self.uid = 0
        self.sempool = []

    def _wait(self, E, toks):
        for key, (sem, v) in toks.items():
            if E is self.PE and sem is self.PE.sem:
                continue
            if E.seen.get(key, 0) >= v:
                continue
            E.e.wait_ge(sem, v)
            E.seen[key] = v

    def _haz(self, E, r, w, p):
        for b in r:
            self._wait(E, b.w)
        for b in w:
            self._wait(E, b.w)
            self._wait(E, b.r)
        for b in p:
            self._wait(E, b.r)

    def _commit(self, tok, r, w, p):
        for b in r:
            _upd(b.r, tok)
        for b in w:
            b.w = {id(tok[0]): tok}
            b.r = {}
        for b in p:
            _upd(b.w, tok)
            b.r = {}

    def op(self, E, fn, r=(), w=(), p=()):
        self._haz(E, r, w, p)
        inst = fn()
        E.cnt += 1
        inst.then_inc(E.sem, 1)
        tok = (E.sem, E.cnt)
        self._commit(tok, r, w, p)
        return tok

    def dma(self, Q, out, in_, sb, r=(), w=(), p=()):
        self._haz(Q, r, w, p)
        if sb.sem is None:
            if self.sempool:
                sb.sem, sb.val = self.sempool.pop()
            else:
                self.nsem += 1
                sb.sem = self.nc.semaphore("ds%d" % self.nsem).__enter__()
        inst = Q.e.dma_start(out=out, in_=in_)
        sb.val += 16
        inst.then_inc(sb.sem, 16)
        tok = (sb.sem, sb.val)
        self._commit(tok, r, w, p)
        return tok

    def barrier(self, bufs):
        allt = {}
        for b in bufs:
            for t in b.w.values():
                _upd(allt, t)
            for t in b.r.values():
                _upd(allt, t)
        for E in self.engs:
            if E.cnt > 0:
                _upd(allt, (E.sem, E.cnt))
        for E in self.engs:
            self._wait(E, allt)

    def retire(self, bufs):
        for b in bufs:
            if b.sem is not None:
                self.sempool.append((b.sem, b.val))
                b.sem = None

    def name(self, s):
        self.uid += 1
        return "%s_%d" % (s, self.uid)


def build_nc(cfg):
    c = derive(cfg)
    D, H, G, F, HALF = c["D"], c["H"], c["G"], c["F"], c["HALF"]
    AW, SW, KD, INW, NST, NKC, NF, T, QR0 = (c[k] for k in ("AW", "SW", "KD", "INW", "NST", "NKC", "NF", "T", "QR0"))
    qchunks = c["qchunks"]
    QRL = T - QR0
    NKT = T // 128
    QT0 = QR0 // 128

    nc = bass.Bass("TRN2", target_bir_lowering=False)
    kb = KB(nc)
    PE, ACT, DVE, POOL, SP = kb.PE, kb.ACT, kb.DVE, kb.POOL, kb.SP
    allbufs = []

    def din(name, shape, dt=F32):
        return Buf(name, nc.dram_tensor(name, list(shape), dt, kind="ExternalInput").ap())

    def dscr(name, shape, dt):
        b = Buf(name, nc.dram_tensor(name, list(shape), dt, kind="Internal").ap())
        allbufs.append(b)
        return b

    x_ext = din("x_ext", [T, D])
    flag_d = din("flag", [128, 1])
    anorm_d = din("anorm", [128, D])
    fnorm_d = din("fnorm", [128, D])
    w_in_d = din("w_in", [D, INW])
    w_out_d = din("w_out", [D, D])
    w_gate_d = din("w_gate", [D, F])
    w_up_d = din("w_up", [D, F])
    w_down_d = din("w_down", [F, D])
    w_glu_d = din("w_glu", [SW, SW])
    qg_d = din("qg", [128, 1])
    kg_d = din("kg", [128, 1])
    lamv_d = din("lamv", [128, 4, 128])
    subg_d = din("subg", [128, 256])
    sAre_d = din("sAre", [128, NST])
    sAim_d = din("sAim", [128, NST])
    sLdt_d = din("sLdt", [128, NST])
    cAre_d = din("cAre", [128, NKC * 64])
    cAim_d = din("cAim", [128, NKC * 64])
    cLdt_d = din("cLdt", [128, NKC * 64])
    BTre_d = din("BTre", [128, NKC * 64])
    BTim_d = din("BTim", [128, NKC * 64])
    CTre_d = din("CTre", [128, NST, 16])
    CTim_d = din("CTim", [128, NST, 16])
    dsk_d = din("dsk", [128, NKC])
    bglu_d = din("bglu", [128, NKC])
    outg_d = din("outg", [128, NKC])
    convw_d = din("convw", [128, 3, NF])
    convb_d = din("convb", [128, NF])
    ident_d = din("ident", [128, 128], BF16)
    cmask_d = din("cmask", [128, 128], BF16)
    negm_d = din("negm", [128, 128], BF16)
    onesb_d = din("onesb", [128, 128], BF16)
    iota_d = din("iota_t", [128, 128])
    rowmask_d = din("rowmask", [128, 8])
    y_out = Buf("y", nc.dram_tensor("y", [HALF, D], F32, kind="ExternalOutput").ap())
    allbufs.append(y_out)

    w_in_b = dscr("w_in_b", [D, INW], BF16)
    w_out_b = dscr("w_out_b", [D, D], BF16)
    w_gate_b = dscr("w_gate_b", [D, F], BF16)
    w_up_b = dscr("w_up_b", [D, F], BF16)
    w_down_b = dscr("w_down_b", [F, D], BF16)
    w_glu_b = dscr("w_glu_b", [SW, SW], BF16)
    kT_s = dscr("kT_s", [2 * H, 128, T], BF16)
    qT_s = dscr("qT_s", [2 * H, 128, T], BF16)
    uT_s = dscr("uT_s", [NKC, 128, T], BF16)
    vv_s = dscr("vv_s", [T, AW], BF16)
    mixT_s = dscr("mixT_s", [KD, 128, T], BF16)
    xmid_s = dscr("xmid_s", [T, D], F32)
    h2T_s = dscr("h2T_s", [KD, 128, T], BF16)
    actT_s = dscr("actT_s", [NF, 128, T], BF16)
    lhsB_s = dscr("lhsB_s", [128, NST, 2, 128], BF16)
    lhsC_s = dscr("lhsC_s", [128, NST, 2, 128], BF16)
    cosT_s = dscr("cosT_s", [128, NST, 128], F32)
    sinT_s = dscr("sinT_s", [128, NST, 128], F32)
    rho_s = dscr("rho_s", [128, NST], F32)
    e128_s = dscr("e128_s", [128, 2, NST], F32)
    y_s = dscr("y_s", [NKC, 128, T], F32)

    FB = [Buf("psf%d" % i, nc.alloc_psum_tensor("psf%d" % i, [128, 512], F32)) for i in range(7)]
    BB = [Buf("psb%d" % i, nc.alloc_psum_tensor("psb%d" % i, [128, 8, 128], BF16)) for i in range(1)]

    def cast_weight(src, dst, rows, cols):
        nchunk = max(1, (rows * cols) // (4 << 20))
        step = (rows + nchunk - 1) // nchunk
        r0 = 0
        while r0 < rows:
            r1 = min(rows, r0 + step)
            kb.dma(POOL, dst.t[r0:r1, :], src.t[r0:r1, :], sb=dst, p=[dst])
            r0 = r1

    regions = [("q", 0, AW), ("k", AW, AW), ("v", 2 * AW, AW), ("u", 3 * AW, SW)]
    w_in_tiles = {}
    for (kind, rs, rw) in regions[1:] + regions[:1]:
        c0 = rs
        while c0 < rs + rw:
            ncols = min(512, rs + rw - c0)
            tb_ = Buf("w_in_b_%d" % c0, w_in_b.t)
            allbufs.append(tb_)
            w_in_tiles[c0] = tb_
            kb.dma(POOL, w_in_b.t[:, c0:c0 + ncols], w_in_d.t[:, c0:c0 + ncols], sb=tb_, p=[tb_])
            c0 += ncols
    cast_weight(w_glu_d, w_glu_b, SW, SW)
    cast_weight(w_out_d, w_out_b, D, D)

    def sbp(name, shape, dt):
        b = Buf(name, nc.alloc_sbuf_tensor(kb.name(name), list(shape), dt))
        allbufs.append(b)
        return b

    ident = sbp("ident", [128, 128], BF16)
    cmask = sbp("cmask", [128, 128], BF16)
    negm = sbp("negm", [128, 128], BF16)
    onesb = sbp("onesb", [128, 128], BF16)
    flag = sbp("flag", [128, 1], F32)
    qgs = sbp("qgs", [128, 1], F32)
    kgs = sbp("kgs", [128, 1], F32)
    lamneg = sbp("lamneg", [128, 1], F32)
    subg = sbp("subg", [128, 256], F32)
    halfpi = sbp("halfpi", [128, 1], F32)
    epsb = sbp("epsb", [128, 1], F32)

    def load(dst, src):
        kb.dma(SP, dst.t[:], src.t[:], sb=dst, w=[dst])

    load(ident, ident_d)
    load(cmask, cmask_d)
    load(negm, negm_d)
    load(onesb, onesb_d)
    load(flag, flag_d)
    load(qgs, qg_d)
    load(kgs, kg_d)
    load(subg, subg_d)
    kb.op(DVE, lambda: nc.vector.memset(halfpi.t[:], PI / 2), w=[halfpi])
    kb.op(DVE, lambda: nc.vector.memset(epsb.t[:], EPS), w=[epsb])
    kb.op(DVE, lambda: nc.vector.tensor_scalar(out=qgs.t[:], in0=qgs.t[:], scalar1=128 ** -0.5, scalar2=None, op0=ALU.mult), w=[qgs])
    kb.op(DVE, lambda: nc.vector.tensor_scalar(out=subg.t[:], in0=subg.t[:], scalar1=0.8, scalar2=None, op0=ALU.mult), w=[subg])

    with ExitStack() as es:
        lamv = Buf("lamv", es.enter_context(nc.sbuf_tensor(kb.name("lamv"), [128, 4, 128], F32)))
        lj = Buf("lj", es.enter_context(nc.sbuf_tensor(kb.name("lj"), [128, 128], F32)))
        l2 = Buf("l2", es.enter_context(nc.sbuf_tensor(kb.name("l2"), [128, 2], F32)))
        load(lamv, lamv_d)
        for i in range(2):
            kb.op(DVE, lambda i=i: nc.vector.tensor_tensor(out=lj.t[:], in0=lamv.t[:, 2 * i, :], in1=lamv.t[:, 2 * i + 1, :], op=ALU.mult), r=[lamv], w=[lj])
            kb.op(DVE, lambda i=i: nc.vector.reduce_sum(out=l2.t[:, i:i + 1], in_=lj.t[:], axis=mybir.AxisListType.X), r=[lj], w=[l2])
        kb.op(ACT, lambda: nc.scalar.activation(out=l2.t[:], in_=l2.t[:], func=AF.Exp), w=[l2])
        kb.op(DVE, lambda: nc.vector.tensor_tensor(out=lamneg.t[:], in0=l2.t[:, 1:2], in1=l2.t[:, 0:1], op=ALU.subtract), r=[l2], w=[lamneg])
        kb.op(DVE, lambda: nc.vector.tensor_scalar(out=lamneg.t[:], in0=lamneg.t[:], scalar1=-0.2, scalar2=None, op0=ALU.add), w=[lamneg])
        kb.barrier([lamv, lj, l2])
        kb.retire([lamv, lj, l2])

    def rstd_from_ss(ss_ap, out_buf, out_ap, n, rbufs):
        kb.op(ACT, lambda: nc.scalar.activation(out=out_ap, in_=ss_ap, func=AF.Sqrt, scale=1.0 / n, bias=epsb.t[:, 0:1]), r=rbufs + [epsb], w=[out_buf])
        kb.op(DVE, lambda: nc.vector.reciprocal(out=out_ap, in_=out_ap), w=[out_buf])

    def transposes(src_buf, src_fn, n, dst_buf, dst_fn, idx0):
        k0 = 0
        i = idx0
        while k0 < n:
            cnt = min(8, n - k0)
            bank = BB[i % len(BB)]

            def f(k0=k0, cnt=cnt, bank=bank):
                inst = None
                for j in range(cnt):
                    inst = nc.tensor.transpose(out=bank.t[:, j, :], in_=src_fn(k0 + j), identity=ident.t[:])
                return inst
            kb.op(PE, f, r=[src_buf, ident], w=[bank])
            E = ACT if (i % 2 == 0) else DVE
            if E is ACT:
                kb.op(ACT, lambda k0=k0, cnt=cnt, bank=bank: nc.scalar.copy(out=dst_fn(k0, cnt), in_=bank.t[:, 0:cnt, :]), r=[bank], p=[dst_buf])
            else:
                kb.op(DVE, lambda k0=k0, cnt=cnt, bank=bank: nc.vector.tensor_copy(out=dst_fn(k0, cnt), in_=bank.t[:, 0:cnt, :]), r=[bank], p=[dst_buf])
            k0 += cnt
            i += 1
        return i

    with ExitStack() as es:
        def sb(name, shape, dt):
            return Buf(name, es.enter_context(nc.sbuf_tensor(kb.name(name), list(shape), dt)))
        NC64 = NKC * 64
        lhsB = sb("lhsB", [128, NST, 2, 128], BF16)
        lhsC = sb("lhsC", [128, NST, 2, 128], BF16)
        cosT = sb("cosT", [128, NST, 128], F32)
        sinT = sb("sinT", [128, NST, 128], F32)
        rho = sb("rho", [128, NST], F32)
        e128 = sb("e128", [128, 2, NST], F32)
        ph = [lhsB, lhsC, cosT, sinT, rho, e128]

        def sincos(es2, ang, n, s_out, c_out, sbuf_, cbuf_):
            def sb2(name, shape, dt):
                return Buf(name, es2.enter_context(nc.sbuf_tensor(kb.name(name), list(shape), dt)))
            ki = sb2("ki", [128, n], I32)
            kf = sb2("kf", [128, n], F32)
            sy = sb2("sy", [128, n], F32)
            cy = sb2("cy", [128, n], F32)
            kb.op(DVE, lambda: nc.vector.tensor_scalar(out=ki.t[:], in0=ang.t[:], scalar1=1.0 / (2 * PI), scalar2=None, op0=ALU.mult), r=[ang], w=[ki])
            kb.op(DVE, lambda: nc.vector.tensor_copy(out=kf.t[:], in_=ki.t[:]), r=[ki], w=[kf])
            kb.op(DVE, lambda: nc.vector.scalar_tensor_tensor(out=ang.t[:], in0=kf.t[:], scalar=-2 * PI, in1=ang.t[:], op0=ALU.mult, op1=ALU.add), r=[kf], w=[ang])
            kb.op(DVE, lambda: nc.vector.tensor_scalar(out=ang.t[:], in0=ang.t[:], scalar1=0.5, scalar2=None, op0=ALU.mult), w=[ang])
            kb.op(DVE, lambda: nc.vector.tensor_scalar(out=ang.t[:], in0=ang.t[:], scalar1=-3.1415925, scalar2=3.1415925, op0=ALU.max, op1=ALU.min), w=[ang])
            kb.op(ACT, lambda: nc.scalar.activation(out=sy.t[:], in_=ang.t[:], func=AF.Sin), r=[ang], w=[sy])
            kb.op(ACT, lambda: nc.scalar.activation(out=kf.t[:], in_=ang.t[:], func=AF.Abs), r=[ang], w=[kf])
            kb.op(ACT, lambda: nc.scalar.activation(out=cy.t[:], in_=kf.t[:], func=AF.Sin, scale=-1.0, bias=halfpi.t[:, 0:1]), r=[kf, halfpi], w=[cy])
            kb.op(DVE, lambda: nc.vector.scalar_tensor_tensor(out=s_out, in0=sy.t[:], scalar=2.0, in1=cy.t[:], op0=ALU.mult, op1=ALU.mult), r=[sy, cy], p=[sbuf_])
            kb.op(DVE, lambda: nc.vector.tensor_tensor(out=kf.t[:], in0=sy.t[:], in1=sy.t[:], op=ALU.mult), r=[sy], w=[kf])
            kb.op(DVE, lambda: nc.vector.tensor_scalar(out=c_out, in0=kf.t[:], scalar1=-2.0, scalar2=1.0, op0=ALU.mult, op1=ALU.add), r=[kf], p=[cbuf_])
            return [ki, kf, sy, cy]

        TT = lambda o, a, b, op: nc.vector.tensor_tensor(out=o, in0=a, in1=b, op=op)
        with ExitStack() as es2:
            def sb2(name, shape, dt):
                return Buf(name, es2.enter_context(nc.sbuf_tensor(kb.name(name), list(shape), dt)))
            cA = sb2("cA", [128, NC64], F32)
            cI = sb2("cI", [128, NC64], F32)
            cL = sb2("cL", [128, NC64], F32)
            bR = sb2("bR", [128, NC64], F32)
            bI = sb2("bI", [128, NC64], F32)
            t1 = sb2("t1", [128, NC64], F32)
            t2 = sb2("t2", [128, NC64], F32)
            t3 = sb2("t3", [128, NC64], F32)
            t4 = sb2("t4", [128, NC64], F32)
            t5 = sb2("t5", [128, NC64], F32)
            t6 = sb2("t6", [128, NC64], F32)
            rowm = sb2("rowm", [128, 8], F32)
            tmpb = [cA, cI, cL, bR, bI, t1, t2, t3, t4, t5, t6, rowm]
            load(cA, cAre_d)
            load(cI, cAim_d)
            load(cL, cLdt_d)
            load(bR, BTre_d)
            load(bI, BTim_d)
            load(rowm, rowmask_d)
            kb.op(ACT, lambda: nc.scalar.activation(out=cL.t[:], in_=cL.t[:], func=AF.Exp), w=[cL])
            kb.op(DVE, lambda: TT(t1.t[:], cA.t[:], cL.t[:], ALU.mult), r=[cA, cL], w=[t1])
            kb.op(ACT, lambda: nc.scalar.activation(out=t1.t[:], in_=t1.t[:], func=AF.Exp), w=[t1])
            kb.op(DVE, lambda: TT(t2.t[:], cI.t[:], cL.t[:], ALU.mult), r=[cI, cL], w=[t2])
            tmpb += sincos(es2, t2, NC64, t3.t[:], t4.t[:], t3, t4)
            kb.op(DVE, lambda: TT(t3.t[:], t3.t[:], t1.t[:], ALU.mult), r=[t1], w=[t3])
            kb.op(DVE, lambda: TT(t4.t[:], t4.t[:], t1.t[:], ALU.mult), r=[t1], w=[t4])
            kb.op(DVE, lambda: nc.vector.tensor_scalar(out=t4.t[:], in0=t4.t[:], scalar1=-1.0, scalar2=None, op0=ALU.add), w=[t4])
            kb.op(DVE, lambda: TT(t1.t[:], cA.t[:], cA.t[:], ALU.mult), r=[cA], w=[t1])
            kb.op(DVE, lambda: TT(t2.t[:], cI.t[:], cI.t[:], ALU.mult), r=[cI], w=[t2])
            kb.op(DVE, lambda: TT(t1.t[:], t1.t[:], t2.t[:], ALU.add), r=[t2], w=[t1])
            kb.op(DVE, lambda: nc.vector.reciprocal(out=t1.t[:], in_=t1.t[:]), w=[t1])
            kb.op(DVE, lambda: TT(t5.t[:], t4.t[:], cA.t[:], ALU.mult), r=[t4, cA], w=[t5])
            kb.op(DVE, lambda: TT(t2.t[:], t3.t[:], cI.t[:], ALU.mult), r=[t3, cI], w=[t2])
            kb.op(DVE, lambda: TT(t5.t[:], t5.t[:], t2.t[:], ALU.add), r=[t2], w=[t5])
            kb.op(DVE, lambda: TT(t5.t[:], t5.t[:], t1.t[:], ALU.mult), r=[t1], w=[t5])
            kb.op(DVE, lambda: TT(t6.t[:], t3.t[:], cA.t[:], ALU.mult), r=[t3, cA], w=[t6])
            kb.op(DVE, lambda: TT(t2.t[:], t4.t[:], cI.t[:], ALU.mult), r=[t4, cI], w=[t2])
            kb.op(DVE, lambda: TT(t6.t[:], t6.t[:], t2.t[:], ALU.subtract), r=[t2], w=[t6])
            kb.op(DVE, lambda: TT(t6.t[:], t6.t[:], t1.t[:], ALU.mult), r=[t1], w=[t6])
            kb.op(DVE, lambda: TT(t3.t[:], t5.t[:], bR.t[:], ALU.mult), r=[t5, bR], w=[t3])
            kb.op(DVE, lambda: TT(t2.t[:], t6.t[:], bI.t[:], ALU.mult), r=[t6, bI], w=[t2])
            kb.op(DVE, lambda: TT(t3.t[:], t3.t[:], t2.t[:], ALU.subtract), r=[t2], w=[t3])
            kb.op(DVE, lambda: TT(t4.t[:], t5.t[:], bI.t[:], ALU.mult), r=[t5, bI], w=[t4])
            kb.op(DVE, lambda: TT(t2.t[:], t6.t[:], bR.t[:], ALU.mult), r=[t6, bR], w=[t2])
            kb.op(DVE, lambda: TT(t4.t[:], t4.t[:], t2.t[:], ALU.add), r=[t2], w=[t4])
            for st in range(NST):
                kc = st // 4
                for gl in range(2):
                    j = (2 * st) % 8 + gl
                    for comp, src in ((0, t3), (1, t4)):
                        kb.op(DVE, lambda st=st, gl=gl, j=j, comp=comp, src=src, kc=kc: nc.vector.tensor_scalar(out=lhsB.t[:, st, comp, gl * 64:(gl + 1) * 64], in0=src.t[:, kc * 64:(kc + 1) * 64], scalar1=rowm.t[:, j:j + 1], scalar2=None, op0=ALU.mult), r=[src, rowm], p=[lhsB])
            kb.barrier(tmpb)
            kb.retire(tmpb)
        with ExitStack() as es2:
            def sb2(name, shape, dt):
                return Buf(name, es2.enter_context(nc.sbuf_tensor(kb.name(name), list(shape), dt)))
            sA = sb2("sA", [128, NST], F32)
            sI = sb2("sI", [128, NST], F32)
            sL = sb2("sL", [128, NST], F32)
            th = sb2("th", [128, NST], F32)
            a128 = sb2("a128", [128, NST], F32)
            iot = sb2("iot", [128, 128], F32)
            angT = sb2("angT", [128, NST * 128], F32)
            cTr = sb2("cTr", [128, NST, 16], F32)
            cTi = sb2("cTi", [128, NST, 16], F32)
            tmpb = [sA, sI, sL, th, a128, iot, angT, cTr, cTi]
            load(sA, sAre_d)
            load(sI, sAim_d)
            load(sL, sLdt_d)
            load(iot, iota_d)
            load(cTr, CTre_d)
            load(cTi, CTim_d)
            kb.op(ACT, lambda: nc.scalar.activation(out=sL.t[:], in_=sL.t[:], func=AF.Exp), w=[sL])
            kb.op(DVE, lambda: TT(rho.t[:], sA.t[:], sL.t[:], ALU.mult), r=[sA, sL], w=[rho])
            kb.op(ACT, lambda: nc.scalar.activation(out=rho.t[:], in_=rho.t[:], func=AF.Exp), w=[rho])
            kb.op(DVE, lambda: TT(th.t[:], sI.t[:], sL.t[:], ALU.mult), r=[sI, sL], w=[th])
            kb.op(DVE, lambda: nc.vector.tensor_scalar(out=a128.t[:], in0=th.t[:], scalar1=128.0, scalar2=None, op0=ALU.mult), r=[th], w=[a128])
            tmpb += sincos(es2, a128, NST, e128.t[:, 1, :], e128.t[:, 0, :], e128, e128)
            for st in range(NST):
                kb.op(DVE, lambda st=st: nc.vector.tensor_scalar(out=angT.t[:, st * 128:(st + 1) * 128], in0=iot.t[:], scalar1=th.t[:, st:st + 1], scalar2=None, op0=ALU.mult), r=[iot, th], p=[angT])
            tmpb += sincos(es2, angT, NST * 128, sinT.t[:].rearrange("p a b -> p (a b)"), cosT.t[:].rearrange("p a b -> p (a b)"), sinT, cosT)
            kb.op(DVE, lambda: nc.vector.memset(lhsC.t[:], 0.0), w=[lhsC])
            for st in range(NST):
                for gl in range(2):
                    j = (2 * st) % 8 + gl
                    kb.op(DVE, lambda st=st, gl=gl, j=j: nc.vector.tensor_copy(out=lhsC.t[gl * 64:(gl + 1) * 64, st, 0, j * 16:(j + 1) * 16], in_=cTr.t[gl * 64:(gl + 1) * 64, st, :]), r=[cTr], p=[lhsC])
                    kb.op(DVE, lambda st=st, gl=gl, j=j: nc.vector.tensor_scalar(out=lhsC.t[gl * 64:(gl + 1) * 64, st, 1, j * 16:(j + 1) * 16], in0=cTi.t[gl * 64:(gl + 1) * 64, st, :], scalar1=-1.0, scalar2=None, op0=ALU.mult), r=[cTi], p=[lhsC])
            kb.barrier(tmpb)
            kb.retire(tmpb)

        for (b_, d_) in ((lhsB, lhsB_s), (lhsC, lhsC_s), (cosT, cosT_s), (sinT, sinT_s), (rho, rho_s), (e128, e128_s)):
            kb.dma(POOL, d_.t[:], b_.t[:], sb=b_, r=[b_], p=[d_])
        kb.barrier(ph)
        kb.retire(ph)

    with ExitStack() as es:
        def sb(name, shape, dt):
            return Buf(name, es.enter_context(nc.sbuf_tensor(kb.name(name), list(shape), dt)))
        hT = sb("hT", [128, KD, 512], BF16)
        anorm = sb("anorm", [128, D], F32)
        xin = [sb("xin", [128, D], F32) for _ in range(2)]
        junk = sb("junk", [128, D], BF16)
        hb = [sb("hb", [128, D], BF16) for _ in range(2)]
        wt = [sb("wt", [128, KD, 512], BF16) for _ in range(2)]
        stg = [sb("stg", [128, 512], BF16) for _ in range(4)]
        sq = [sb("sq", [128, 512], BF16) for _ in range(2)]
        rinv = [sb("rinv", [128, 512], F32) for _ in range(2)]
        ssq = [sb("ssq", [128, 1], F32) for _ in range(2)]
        ph = [hT, anorm, junk] + xin + hb + wt + stg + sq + rinv + ssq
        load(anorm, anorm_d)

        n_x = n_tp = n_w = n_acc = n_stg = n_qk = 0
        for ch in range(T // 512):
            t0 = ch * 512
            for sub in range(4):
                xi = xin[n_x % 2]
                hbb = hb[n_x % 2]
                sqb = ssq[n_x % 2]
                n_x += 1
                r0 = t0 + sub * 128
                kb.dma(SP, xi.t[:], x_ext.t[r0:r0 + 128, :], sb=xi, w=[xi])
                kb.op(ACT, lambda xi=xi, sqb=sqb: nc.scalar.activation(out=junk.t[:], in_=xi.t[:], func=AF.Square, accum_out=sqb.t[:, 0:1]), r=[xi], w=[junk, sqb])
                rstd_from_ss(sqb.t[:, 0:1], sqb, sqb.t[:, 0:1], D, [sqb])
                kb.op(DVE, lambda xi=xi, sqb=sqb, hbb=hbb: nc.vector.scalar_tensor_tensor(out=hbb.t[:], in0=xi.t[:], scalar=sqb.t[:, 0:1], in1=anorm.t[:], op0=ALU.mult, op1=ALU.mult), r=[xi, sqb, anorm], w=[hbb])
                n_tp = transposes(hbb, lambda k, hbb=hbb: hbb.t[:, k * 128:(k + 1) * 128], KD, hT,
                                  lambda k0, cnt, sub=sub: hT.t[:, k0:k0 + cnt, sub * 128:(sub + 1) * 128], n_tp)
            for (kind, rs, rw) in regions:
                if kind == "q" and t0 + 512 <= QR0:
                    continue
                c0 = rs
                while c0 < rs + rw:
                    ncols = min(512, rs + rw - c0)
                    wtile = wt[n_w % 2]
                    n_w += 1
                    kb.dma(SP, wtile.t[:, :, 0:ncols], w_in_b.t[:, c0:c0 + ncols].rearrange("(k p) c -> p k c", p=128), sb=wtile, r=[w_in_tiles[c0]], w=[wtile])
                    if kind == "v":
                        for sub in range(4):
                            acc = FB[n_acc % 4]
                            n_acc += 1

                            def f(acc=acc, sub=sub, wtile=wtile, ncols=ncols):
                                inst = None
                                for k in range(KD):
                                    inst = nc.tensor.matmul(acc.t[:, 0:ncols], lhsT=hT.t[:, k, sub * 128:(sub + 1) * 128], rhs=wtile.t[:, k, 0:ncols], start=(k == 0), stop=(k == KD - 1))
                                return inst
                            kb.op(PE, f, r=[hT, wtile], w=[acc])
                            st_ = stg[n_stg % 4]
                            n_stg += 1
                            if n_stg % 2 == 0:
                                kb.op(ACT, lambda acc=acc, st_=st_, ncols=ncols: nc.scalar.copy(out=st_.t[:, 0:ncols], in_=acc.t[:, 0:ncols]), r=[acc], w=[st_])
                            else:
                                kb.op(DVE, lambda acc=acc, st_=st_, ncols=ncols: nc.vector.tensor_copy(out=st_.t[:, 0:ncols], in_=acc.t[:, 0:ncols]), r=[acc], w=[st_])
                            r0 = t0 + sub * 128
                            kb.dma(POOL, vv_s.t[r0:r0 + 128, c0 - rs:c0 - rs + ncols], st_.t[:, 0:ncols], sb=st_, r=[st_], p=[vv_s])
                    else:
                        for jb in range(ncols // 128):
                            blk = (c0 - rs) // 128 + jb
                            acc = FB[n_acc % 4]
                            n_acc += 1

                            def f(acc=acc, jb=jb, wtile=wtile):
                                inst = None
                                for k in range(KD):
                                    inst = nc.tensor.matmul(acc.t[:, :], lhsT=wtile.t[:, k, jb * 128:(jb + 1) * 128], rhs=hT.t[:, k, :], start=(k == 0), stop=(k == KD - 1))
                                return inst
                            kb.op(PE, f, r=[hT, wtile], w=[acc])
                            st_ = stg[n_stg % 4]
                            n_stg += 1
                            if kind == "u":
                                kb.op(ACT, lambda acc=acc, st_=st_: nc.scalar.copy(out=st_.t[:], in_=acc.t[:]), r=[acc], w=[st_])
                                kb.dma(POOL, uT_s.t[blk, :, t0:t0 + 512], st_.t[:], sb=st_, r=[st_], p=[uT_s])
                            else:
                                sqb = sq[n_qk % 2]
                                rb = rinv[n_qk % 2]
                                ssb = FB[4 + n_qk % 2]
                                n_qk += 1
                                kb.op(ACT, lambda acc=acc, sqb=sqb: nc.scalar.activation(out=sqb.t[:], in_=acc.t[:], func=AF.Square), r=[acc], w=[sqb])
                                kb.op(PE, lambda ssb=ssb, sqb=sqb: nc.tensor.matmul(ssb.t[:], lhsT=onesb.t[:], rhs=sqb.t[:], start=True, stop=True), r=[sqb, onesb], w=[ssb])
                                rstd_from_ss(ssb.t[:], rb, rb.t[:], 128, [ssb])
                                gn = qgs if kind == "q" else kgs
                                kb.op(DVE, lambda acc=acc, st_=st_, rb=rb, gn=gn: nc.vector.scalar_tensor_tensor(out=st_.t[:], in0=acc.t[:], scalar=gn.t[:, 0:1], in1=rb.t[:], op0=ALU.mult, op1=ALU.mult), r=[acc, rb, gn], w=[st_])
                                dst = qT_s if kind == "q" else kT_s
                                kb.dma(POOL, dst.t[blk, :, t0:t0 + 512], st_.t[:], sb=st_, r=[st_], p=[dst])
                    c0 += ncols
        kb.barrier(ph + allbufs + FB + BB)
        kb.retire(ph)

    with ExitStack() as es:
        def sb(name, shape, dt):
            return Buf(name, es.enter_context(nc.sbuf_tensor(kb.name(name), list(shape), dt)))
        TT = lambda o, a, b, op: nc.vector.tensor_tensor(out=o, in0=a, in1=b, op=op)
        GT = lambda o, a_, b_, op: nc.gpsimd.tensor_tensor(out=o, in0=a_, in1=b_, op=op)
        kTt = [sb("kTt", [128, 2, T], BF16) for _ in range(2)]
        Vt = [sb("Vt", [128, NKT, 257], BF16) for _ in range(1)]
        qTt = [sb("qTt", [128, 2, QRL], BF16) for _ in range(2)]
        pt = [sb("pt", [128, 512], BF16) for _ in range(3)]
        a1s = [sb("a1", [128, 4, 256], F32) for _ in range(2)]
        av = [sb("av", [128, 256], F32) for _ in range(2)]
        ab = [sb("ab", [128, 256], BF16) for _ in range(2)]
        aj = sb("aj", [128, 256], BF16)
        aTst = [sb("aTst", [128, 2, 512], BF16) for _ in range(2)]
        rl = [sb("rl", [128, 2], F32) for _ in range(4)]
        osb = [sb("osb", [128, 4, 257], F32) for _ in range(3)]
        xs = [sb("xs", [128, 256], F32) for _ in range(2)]
        ph = kTt + Vt + qTt + pt + [aj] + a1s + av + ab + aTst + rl + osb + xs
        NH = min(8, NST)
        NQ = NST // NH
        lhsB = sb("lhsB", [128, NST, 2, 128], BF16)
        lhsC = sb("lhsC", [128, NST, 2, 128], BF16)
        cosT = sb("cosT", [128, NST, 128], F32)
        sinT = sb("sinT", [128, NST, 128], F32)
        rho = sb("rho", [128, NST], F32)
        e128 = sb("e128", [128, 2, NST], F32)
        dsk = sb("dsk", [128, NKC], F32)
        ini = [sb("ini", [128, 2, NST], F32) for _ in range(2)]
        ut = [sb("ut", [128, NKC, 128], BF16) for _ in range(2)]
        vr = sb("vr", [128, NH, 128], F32)
        vi = sb("vi", [128, NH, 128], F32)
        gr = sb("gr", [128, NH, 128], F32)
        gi = sb("gi", [128, NH, 128], F32)
        hbr = [sb("hbr", [128, NH, 128], BF16) for _ in range(2)]
        hbi = [sb("hbi", [128, NH, 128], BF16) for _ in range(2)]
        pa = sb("pa", [128, 512], F32)
        pb2 = sb("pb2", [128, 512], F32)
        ta = sb("ta", [128, 128], F32)
        tb = sb("tb", [128, 128], F32)
        it4 = sb("it4", [128, 4, NH], F32)
        yst = [sb("yst", [128, NKC, 128], F32) for _ in range(2)]
        XB = FB[6]
        YB = FB[6]
        ph += [lhsB, lhsC, cosT, sinT, rho, e128, dsk, vr, vi, gr, gi, pa, pb2, ta, tb, it4] + ini + ut + hbr + hbi + yst
        for (b_, d_) in ((lhsB, lhsB_s), (lhsC, lhsC_s), (cosT, cosT_s), (sinT, sinT_s), (rho, rho_s), (e128, e128_s)):
            kb.dma(SP, b_.t[:], d_.t[:], sb=b_, r=[d_], w=[b_])
        load(dsk, dsk_d)
        for v_ in Vt:
            kb.op(DVE, lambda v_=v_: nc.vector.memset(v_.t[:, :, 256:257], 1.0), p=[v_])
            kb.op(DVE, lambda v_=v_: nc.vector.tensor_copy(out=v_.t[:, 0:HALF // 128, 256], in_=flag.t[:, 0:1].to_broadcast([128, HALF // 128])), r=[flag], p=[v_])

        def ssm_gen():
            kb.op(DVE, lambda: nc.vector.memset(ini[0].t[:], 0.0), w=[ini[0]])
            fifo = []
            nx = 0
            nq = 0
            for ti in range(T // 128):
                t0 = ti * 128
                u_ = ut[ti % 2]
                kb.dma(SP, u_.t[:], uT_s.t[:, :, t0:t0 + 128].rearrange("k p t -> p k t"), sb=u_, r=[uT_s], w=[u_])
                icur, inxt = ini[ti % 2], ini[(ti + 1) % 2]
                qtile = ti >= QT0
                yst_ = yst[ti % 2]
                for qr in range(NQ):
                    st0 = qr * NH
                    for sl in range(NH):
                        st = st0 + sl
                        xb = XB
                        xs_ = xs[nx % 2]
                        nx += 1

                        def f():
                            nc.tensor.matmul(xb.t[:, 0:128], lhsT=lhsB.t[:, st, 0, :], rhs=u_.t[:, st // 4, :], start=True, stop=True)
                            return nc.tensor.matmul(xb.t[:, 128:256], lhsT=lhsB.t[:, st, 1, :], rhs=u_.t[:, st // 4, :], start=True, stop=True)
                        kb.op(PE, f, r=[lhsB, u_], w=[xb])
                        kb.op(ACT, lambda: nc.scalar.copy(out=xs_.t[:], in_=xb.t[:, 0:256]), r=[xb], w=[xs_])
                        xr, xi_ = xs_.t[:, 0:128], xs_.t[:, 128:256]
                        c_, s_ = cosT.t[:, st, :], sinT.t[:, st, :]
                        if qtile:
                            E1, X1, tA, tB, tAb, tBb = DVE, TT, ta.t[:], tb.t[:], ta, tb
                        else:
                            E1, X1, tA, tB, tAb, tBb = POOL, GT, pa.t[:, 0:128], pb2.t[:, 0:128], pa, pb2
                        kb.op(E1, lambda: X1(vr.t[:, sl, :], xr, c_, ALU.mult), r=[xs_, cosT], p=[vr])
                        kb.op(E1, lambda: X1(tA, xi_, s_, ALU.mult), r=[xs_, sinT], w=[tAb])
                        kb.op(E1, lambda: X1(vi.t[:, sl, :], xi_, c_, ALU.mult), r=[xs_, cosT], p=[vi])
                        kb.op(E1, lambda: X1(tB, xr, s_, ALU.mult), r=[xs_, sinT], w=[tBb])
                        kb.op(E1, lambda: X1(vr.t[:, sl, :], vr.t[:, sl, :], tA, ALU.add), r=[tAb], p=[vr])
                        kb.op(E1, lambda: X1(vi.t[:, sl, :], vi.t[:, sl, :], tB, ALU.subtract), r=[tBb], p=[vi])
                        yield
                    while fifo:
                        fifo.pop(0)()
                    for sl in range(NH):
                        st = st0 + sl
                        for comp, (src, dst) in enumerate(((vr, gr), (vi, gi))):
                            kb.op(DVE, lambda: nc.vector.tensor_tensor_scan(out=dst.t[:, sl, :], data0=rho.t[:, st:st + 1].to_broadcast([128, 128]), data1=src.t[:, sl, :], initial=icur.t[:, comp, st:st + 1], op0=ALU.mult, op1=ALU.add), r=[src, rho, icur], p=[dst])
                        if sl % 2 == 1:
                            yield
                    gre, gie = gr.t[:, :, 127], gi.t[:, :, 127]
                    c8, s8 = e128.t[:, 0, st0:st0 + NH], e128.t[:, 1, st0:st0 + NH]
                    kb.op(DVE, lambda: TT(it4.t[:, 0, :], gre, c8, ALU.mult), r=[gr, e128], p=[it4])
                    kb.op(DVE, lambda: TT(it4.t[:, 1, :], gie, s8, ALU.mult), r=[gi, e128], p=[it4])
                    kb.op(DVE, lambda: TT(it4.t[:, 2, :], gre, s8, ALU.mult), r=[gr, e128], p=[it4])
                    kb.op(DVE, lambda: TT(it4.t[:, 3, :], gie, c8, ALU.mult), r=[gi, e128], p=[it4])
                    kb.op(DVE, lambda: TT(inxt.t[:, 0, st0:st0 + NH], it4.t[:, 0, :], it4.t[:, 1, :], ALU.subtract), r=[it4], p=[inxt])
                    kb.op(DVE, lambda: TT(inxt.t[:, 1, st0:st0 + NH], it4.t[:, 2, :], it4.t[:, 3, :], ALU.add), r=[it4], p=[inxt])
                    if qtile:
                        hr_, hi_ = hbr[nq % 2], hbi[nq % 2]
                        nq += 1
                        for hh in range(NH // 4):
                            sl4 = slice(hh * 4, hh * 4 + 4)
                            st4 = slice(st0 + hh * 4, st0 + hh * 4 + 4)
                            q4 = lambda b_: b_.t[:, sl4, :].rearrange("p a b -> p (a b)")
                            t4 = lambda b_: b_.t[:, st4, :].rearrange("p a b -> p (a b)")
                            kb.op(POOL, lambda: GT(pa.t[:], q4(gr), t4(cosT), ALU.mult), r=[gr, cosT], w=[pa])
                            kb.op(POOL, lambda: GT(pb2.t[:], q4(gi), t4(sinT), ALU.mult), r=[gi, sinT], w=[pb2])
                            kb.op(POOL, lambda: GT(q4(hr_), pa.t[:], pb2.t[:], ALU.subtract), r=[pa, pb2], p=[hr_])
                            kb.op(POOL, lambda: GT(pa.t[:], q4(gr), t4(sinT), ALU.mult), r=[gr, sinT], w=[pa])
                            kb.op(POOL, lambda: GT(pb2.t[:], q4(gi), t4(cosT), ALU.mult), r=[gi, cosT], w=[pb2])
                            kb.op(POOL, lambda: GT(q4(hi_), pa.t[:], pb2.t[:], ALU.add), r=[pa, pb2], p=[hi_])

                        def cstage(st0=st0, hr_=hr_, hi_=hi_, u_=u_, yst_=yst_):
                            for kcl in range(NH // 4):
                                kc = st0 // 4 + kcl

                                def f():
                                    inst = None
                                    for s4 in range(4):
                                        sl_ = kcl * 4 + s4
                                        st_ = st0 + sl_
                                        nc.tensor.matmul(YB.t[:, 256:384], lhsT=lhsC.t[:, st_, 0, :], rhs=hr_.t[:, sl_, :], start=(s4 == 0), stop=False)
                                        inst = nc.tensor.matmul(YB.t[:, 256:384], lhsT=lhsC.t[:, st_, 1, :], rhs=hi_.t[:, sl_, :], start=False, stop=(s4 == 3))
                                    return inst
                                kb.op(PE, f, r=[lhsC, hr_, hi_], w=[YB])
                                kb.op(DVE, lambda: nc.vector.scalar_tensor_tensor(out=yst_.t[:, kc, :], in0=u_.t[:, kc, :], scalar=dsk.t[:, kc:kc + 1], in1=YB.t[:, 256:384], op0=ALU.mult, op1=ALU.add), r=[u_, dsk, YB], p=[yst_])
                        fifo.append(cstage)
                        if qr == NQ - 1:
                            fifo.append(lambda t0=t0, yst_=yst_: kb.dma(POOL, y_s.t[:, :, t0:t0 + 128].rearrange("k p t -> p k t"), yst_.t[:], sb=yst_, r=[yst_], p=[y_s]))
                    yield
            while fifo:
                fifo.pop(0)()
                yield

        n_slices = (T // 128) * NQ * (NH + NH // 2 + 1) + 4
        n_its = H * sum(2 * (t0 // 128 + nt // 128) for (t0, nt) in qchunks)
        rate = RATE_SCALE * n_slices / n_its
        sg = ssm_gen()
        pull = dict(acc=0.0)
        cnt = dict(s=0, pt=0, rl=0, av=0, tp=0, ch=0, g=0)
        for h in range(H):
            sl = h % 2
            kt_, vt_, qt_ = kTt[sl], Vt[0], qTt[sl]
            for (src_, dst_, rows_) in ((w_gate_d, w_gate_b, D), (w_up_d, w_up_b, D), (w_down_d, w_down_b, F)):
                ra, rb_ = (rows_ * h) // H, (rows_ * (h + 1)) // H
                kb.dma(POOL, dst_.t[ra:rb_, :], src_.t[ra:rb_, :], sb=dst_, p=[dst_])
            for m in range(2):
                kb.dma(SP, kt_.t[:, m, :], kT_s.t[2 * h + m, :, :], sb=kt_, r=[kT_s], p=[kt_])
                kb.dma(SP, qt_.t[:, m, :], qT_s.t[2 * h + m, :, QR0:T], sb=qt_, r=[qT_s], p=[qt_])
            kb.dma(SP, vt_.t[:, :, 0:256], vv_s.t[:, h * 256:(h + 1) * 256].rearrange("(k p) e -> p k e", p=128), sb=vt_, r=[vv_s], p=[vt_])
            its = []
            for (t0, nt) in qchunks:
                for m in range(2):
                    for kt in range(t0 // 128 + nt // 128):
                        its.append((t0, nt, m, kt))

            def emit_S(it):
                t0, nt, m, kt = it
                qt0 = t0 // 128
                qo = t0 - QR0
                j0 = max(0, kt - qt0)
                S = FB[4 + cnt["s"] % 2]
                cnt["s"] += 1
                diag = kt >= qt0

                def fs():
                    inst = nc.tensor.matmul(S.t[:, j0 * 128:nt], lhsT=kt_.t[:, m, kt * 128:(kt + 1) * 128], rhs=qt_.t[:, m, qo + j0 * 128:qo + nt], start=True, stop=not diag)
                    if diag:
                        inst = nc.tensor.matmul(S.t[:, j0 * 128:(j0 + 1) * 128], lhsT=negm.t[:], rhs=ident.t[:], start=False, stop=True)
                    return inst
                kb.op(PE, fs, r=[kt_, qt_, negm, ident], w=[S])
                pb = pt[cnt["pt"] % 3]
                cnt["pt"] += 1
                kb.op(ACT, lambda: nc.scalar.activation(out=pb.t[:, j0 * 128:nt], in_=S.t[:, j0 * 128:nt], func=AF.Exp), r=[S], w=[pb])
                return pb

            def emit_PV(it, pb):
                t0, nt, m, kt = it
                ns = nt // 128
                qt0 = t0 // 128
                j0 = max(0, kt - qt0)
                O = FB[0:ns]

                def f():
                    inst = None
                    for j in range(j0, ns):
                        inst = nc.tensor.matmul(O[j].t[:, 0:257], lhsT=pb.t[:, j * 128:(j + 1) * 128], rhs=vt_.t[:, kt, :], start=(kt == 0), stop=(kt == qt0 + j))
                    return inst
                if kt == 0:
                    kb.op(PE, f, r=[pb, vt_], w=O[j0:ns])
                else:
                    kb.op(PE, f, r=[pb, vt_], p=O[j0:ns])
                if kt != qt0 + ns - 1:
                    return
                ob = osb[cnt["g"] % 3]
                cnt["g"] += 1
                for j in range(ns):
                    kb.op(ACT, lambda j=j: nc.scalar.copy(out=ob.t[:, j, :], in_=O[j].t[:, 0:257]), r=[O[j]], p=[ob])
                if m == 0:
                    cnt["ch"] += 1
                ast = aTst[cnt["ch"] % 2]
                a1 = a1s[cnt["ch"] % 2]
                for j in range(ns):
                    rlb = rl[cnt["rl"] % 4]
                    cnt["rl"] += 1
                    kb.op(DVE, lambda: nc.vector.tensor_scalar(out=rlb.t[:, 0:1], in0=ob.t[:, j, 256:257], scalar1=1e-30, scalar2=None, op0=ALU.add), r=[ob], w=[rlb])
                    kb.op(DVE, lambda: nc.vector.reciprocal(out=rlb.t[:, 0:1], in_=rlb.t[:, 0:1]), w=[rlb])
                    if m == 0:
                        kb.op(ACT, lambda: nc.scalar.activation(out=a1.t[:, j, :], in_=ob.t[:, j, 0:256], func=AF.Copy, scale=rlb.t[:, 0:1]), r=[ob, rlb], p=[a1])
                    else:
                        avb = av[cnt["av"] % 2]
                        abb = ab[cnt["av"] % 2]
                        cnt["av"] += 1
                        kb.op(DVE, lambda: nc.vector.tensor_scalar(out=rlb.t[:, 0:1], in0=rlb.t[:, 0:1], scalar1=lamneg.t[:, 0:1], scalar2=None, op0=ALU.mult), r=[lamneg], w=[rlb])
                        kb.op(DVE, lambda: nc.vector.scalar_tensor_tensor(out=avb.t[:], in0=ob.t[:, j, 0:256], scalar=rlb.t[:, 0:1], in1=a1.t[:, j, :], op0=ALU.mult, op1=ALU.add), r=[ob, rlb, a1], w=[avb])
                        kb.op(ACT, lambda: nc.scalar.activation(out=aj.t[:], in_=avb.t[:], func=AF.Square, accum_out=rlb.t[:, 1:2]), r=[avb], w=[aj, rlb])
                        rstd_from_ss(rlb.t[:, 1:2], rlb, rlb.t[:, 1:2], 256, [rlb])
                        kb.op(DVE, lambda: nc.vector.scalar_tensor_tensor(out=abb.t[:], in0=avb.t[:], scalar=rlb.t[:, 1:2], in1=subg.t[:], op0=ALU.mult, op1=ALU.mult), r=[avb, rlb, subg], w=[abb])
                        cnt["tp"] = transposes(abb, lambda k: abb.t[:, k * 128:(k + 1) * 128], 2, ast,
                                               lambda k0, c_: ast.t[:, k0:k0 + c_, j * 128:(j + 1) * 128], cnt["tp"])
                if m == 1:
                    kb.dma(POOL, mixT_s.t[2 * h:2 * h + 2, :, t0:t0 + nt].rearrange("k p t -> p k t"), ast.t[:, :, 0:nt], sb=ast, r=[ast], p=[mixT_s])

            cur = emit_S(its[0])
            for i, it in enumerate(its):
                nxt = emit_S(its[i + 1]) if i + 1 < len(its) else None
                emit_PV(it, cur)
                cur = nxt
                pull["acc"] += rate
                while pull["acc"] >= 1.0:
                    pull["acc"] -= 1.0
                    next(sg, None)
        for _ in sg:
            pass
        kb.barrier(ph + allbufs + FB + BB)
        kb.retire(ph)

    with ExitStack() as es:
        def sb(name, shape, dt):
            return Buf(name, es.enter_context(nc.sbuf_tensor(kb.name(name), list(shape), dt)))
        TT = lambda o, a, b, op: nc.vector.tensor_tensor(out=o, in0=a, in1=b, op=op)
        wglu = sb("wglu", [128, NKC, SW], BF16)
        bglu = sb("bglu", [128, NKC], F32)
        outg = sb("outg", [128, NKC], F32)
        yv = sb("yv", [128, NKC, 512], F32)
        yt1 = sb("yt1", [128, NKC, 512], F32)
        ygf = sb("ygf", [128, NKC, 512], F32)
        ygb = sb("ygb", [128, NKC, 512], BF16)
        sgz = sb("sgz", [128, NKC, 512], F32)
        yo = sb("yo", [128, NKC, 512], F32)
        ysq = sb("ysq", [128, NKC, 512], BF16)
        rr = sb("rr", [128, 512], F32)
        so = [sb("so", [128, NKC, 512], BF16) for _ in range(2)]
        ph = [wglu, bglu, outg, yv, yt1, ygf, ygb, sgz, yo, ysq, rr] + so
        load(bglu, bglu_d)
        load(outg, outg_d)
        kb.dma(SP, wglu.t[:], w_glu_b.t[:, :].rearrange("(k p) c -> p k c", p=128), sb=wglu, r=[w_glu_b], w=[wglu])
        for ci, (t0, nt) in enumerate(qchunks):
            V = lambda b_: b_.t[:, :, 0:nt]
            kb.dma(SP, V(yv), y_s.t[:, :, t0:t0 + nt].rearrange("k p t -> p k t"), sb=yv, r=[y_s], w=[yv])
            kb.op(DVE, lambda: TT(V(yt1), V(yv), V(yv), ALU.mult), r=[yv], w=[yt1])
            kb.op(DVE, lambda: nc.vector.tensor_scalar(out=V(yt1), in0=V(yt1), scalar1=0.044715, scalar2=1.0, op0=ALU.mult, op1=ALU.add), w=[yt1])
            kb.op(DVE, lambda: TT(V(yt1), V(yt1), V(yv), ALU.mult), r=[yv], w=[yt1])
            kb.op(ACT, lambda: nc.scalar.activation(out=V(yt1), in_=V(yt1), func=AF.Sigmoid, scale=1.5957691216), w=[yt1])
            kb.op(DVE, lambda: TT(V(ygf), V(yt1), V(yv), ALU.mult), r=[yt1, yv], w=[ygf])
            kb.op(ACT, lambda: nc.scalar.copy(out=V(ygb), in_=V(ygf)), r=[ygf], w=[ygb])
            for jc in range(NKC):
                zb = FB[jc % 4]

                def f():
                    inst = None
                    for kc in range(NKC):
                        inst = nc.tensor.matmul(zb.t[:, 0:nt], lhsT=wglu.t[:, kc, jc * 128:(jc + 1) * 128], rhs=ygb.t[:, kc, 0:nt], start=(kc == 0), stop=(kc == NKC - 1))
                    return inst
                kb.op(PE, f, r=[wglu, ygb], w=[zb])
                kb.op(ACT, lambda: nc.scalar.activation(out=sgz.t[:, jc, 0:nt], in_=zb.t[:, 0:nt], func=AF.Sigmoid, bias=bglu.t[:, jc:jc + 1]), r=[zb, bglu], p=[sgz])
            kb.op(DVE, lambda: TT(V(yo), V(ygf), V(sgz), ALU.mult), r=[ygf, sgz], w=[yo])
            kb.op(ACT, lambda: nc.scalar.activation(out=V(ysq), in_=V(yo), func=AF.Square), r=[yo], w=[ysq])
            sb_ = FB[4]

            def f():
                inst = None
                for kc in range(NKC):
                    inst = nc.tensor.matmul(sb_.t[:, 0:nt], lhsT=onesb.t[:], rhs=ysq.t[:, kc, 0:nt], start=(kc == 0), stop=(kc == NKC - 1))
                return inst
            kb.op(PE, f, r=[onesb, ysq], w=[sb_])
            rstd_from_ss(sb_.t[:, 0:nt], rr, rr.t[:, 0:nt], SW, [sb_])
            so_ = so[ci % 2]
            for kc in range(NKC):
                kb.op(DVE, lambda: nc.vector.scalar_tensor_tensor(out=so_.t[:, kc, 0:nt], in0=yo.t[:, kc, 0:nt], scalar=outg.t[:, kc:kc + 1], in1=rr.t[:, 0:nt], op0=ALU.mult, op1=ALU.mult), r=[yo, outg, rr], p=[so_])
            kb.dma(POOL, mixT_s.t[AW // 128:KD, :, t0:t0 + nt].rearrange("k p t -> p k t"), so_.t[:, :, 0:nt], sb=so_, r=[so_], p=[mixT_s])
        kb.barrier(ph + allbufs + FB + BB)
        kb.retire(ph)

    with ExitStack() as es:
        def sb(name, shape, dt):
            return Buf(name, es.enter_context(nc.sbuf_tensor(kb.name(name), list(shape), dt)))
        fnorm = sb("fnorm", [128, D], F32)
        mixc = sb("mixc", [128, KD, 512], BF16)
        wt = [sb("wt", [128, KD, 256], BF16) for _ in range(2)]
        xm = sb("xm", [128, 4, D], F32)
        xp = [sb("xp", [128, 256], F32) for _ in range(4)]
        junk = sb("junk", [128, D], BF16)
        hb = sb("hb", [128, D], BF16)
        h2t = [sb("h2t", [128, KD, 128], BF16) for _ in range(2)]
        ssq = [sb("ssq", [128, 1], F32) for _ in range(2)]
        ph = [fnorm, junk, mixc, xm, hb] + wt + xp + h2t + ssq
        load(fnorm, fnorm_d)
        n_w = n_acc = n_tp = n_xp = n_t = 0
        for (t0, nt) in qchunks:
            ns = nt // 128
            kb.dma(SP, mixc.t[:, :, 0:nt], mixT_s.t[:, :, t0:t0 + nt].rearrange("k p t -> p k t"), sb=mixc, r=[mixT_s], w=[mixc])
            for cb in range(D // 256):
                wtile = wt[n_w % 2]
                n_w += 1
                kb.dma(SP, wtile.t[:], w_out_b.t[:, cb * 256:(cb + 1) * 256].rearrange("(k p) c -> p k c", p=128), sb=wtile, r=[w_out_b], w=[wtile])
                accs = [FB[(n_acc + j) % 6] for j in range(ns)]
                n_acc += ns

                def f(accs=accs, wtile=wtile, ns=ns):
                    inst = None
                    for k in range(KD):
                        for j in range(ns):
                            inst = nc.tensor.matmul(accs[j].t[:, 0:256], lhsT=mixc.t[:, k, j * 128:(j + 1) * 128], rhs=wtile.t[:, k, :], start=(k == 0), stop=(k == KD - 1))
                    return inst
                kb.op(PE, f, r=[mixc, wtile], w=accs)
                for j in range(ns):
                    xp_ = xp[n_xp % 4]
                    n_xp += 1
                    r0 = t0 + j * 128
                    kb.dma(SP, xp_.t[:], x_ext.t[r0:r0 + 128, cb * 256:(cb + 1) * 256], sb=xp_, w=[xp_])
                    kb.op(DVE, lambda j=j, xp_=xp_, accs=accs, cb=cb: nc.vector.tensor_tensor(out=xm.t[:, j, cb * 256:(cb + 1) * 256], in0=accs[j].t[:, 0:256], in1=xp_.t[:], op=ALU.add), r=[accs[j], xp_], p=[xm])
            for j in range(ns):
                r0 = t0 + j * 128
                h2_, sq_ = h2t[n_t % 2], ssq[n_t % 2]
                n_t += 1
                kb.dma(POOL, xmid_s.t[r0:r0 + 128, :], xm.t[:, j, :], sb=xm, r=[xm], p=[xmid_s])
                kb.op(ACT, lambda j=j, sq_=sq_: nc.scalar.activation(out=junk.t[:], in_=xm.t[:, j, :], func=AF.Square, accum_out=sq_.t[:, 0:1]), r=[xm], w=[junk, sq_])
                rstd_from_ss(sq_.t[:, 0:1], sq_, sq_.t[:, 0:1], D, [sq_])
                kb.op(DVE, lambda j=j, sq_=sq_: nc.vector.scalar_tensor_tensor(out=hb.t[:], in0=xm.t[:, j, :], scalar=sq_.t[:, 0:1], in1=fnorm.t[:], op0=ALU.mult, op1=ALU.mult), r=[xm, sq_, fnorm], w=[hb])
                n_tp = transposes(hb, lambda k: hb.t[:, k * 128:(k + 1) * 128], KD, h2_,
                                  lambda k0, c_, h2_=h2_: h2_.t[:, k0:k0 + c_, :], n_tp)
                kb.dma(POOL, h2T_s.t[:, :, r0:r0 + 128].rearrange("k p t -> p k t"), h2_.t[:], sb=h2_, r=[h2_], p=[h2T_s])
        kb.barrier(ph + allbufs + FB + BB)
        kb.retire(ph)

    with ExitStack() as es:
        def sb(name, shape, dt):
            return Buf(name, es.enter_context(nc.sbuf_tensor(kb.name(name), list(shape), dt)))
        h2c = sb("h2c", [128, KD, 512], BF16)
        wg = [sb("wg", [128, KD, 512], BF16) for _ in range(2)]
        wu = [sb("wu", [128, KD, 512], BF16) for _ in range(2)]
        gh = sb("gh", [128, NF, 2], F32)
        cw = sb("cw", [128, 3, NF], F32)
        cbs = sb("cbs", [128, NF], F32)
        gs = [sb("gs", [128, 514], F32) for _ in range(2)]
        tm = [sb("tm", [128, 512], F32) for _ in range(2)]
        tm2 = [sb("tm2", [128, 512], F32) for _ in range(2)]
        ast = [sb("ast", [128, 512], BF16) for _ in range(3)]
        ph = [h2c, gh, cw, cbs] + wg + wu + gs + tm + tm2 + ast
        load(cw, convw_d)
        load(cbs, convb_d)
        n_w = n_f = 0
        for ci, (t0, nt) in enumerate(qchunks):
            halo = (ci == 0)
            kb.dma(SP, h2c.t[:, :, 0:nt], h2T_s.t[:, :, t0:t0 + nt].rearrange("k p t -> p k t"), sb=h2c, r=[h2T_s], w=[h2c])
            c0 = 0
            while c0 < F:
                ncols = min(512, F - c0)
                wg_, wu_ = wg[n_w % 2], wu[n_w % 2]
                n_w += 1
                kb.dma(SP, wg_.t[:, :, 0:ncols], w_gate_b.t[:, c0:c0 + ncols].rearrange("(k p) c -> p k c", p=128), sb=wg_, r=[w_gate_b], w=[wg_])
                if not halo:
                    kb.dma(SP, wu_.t[:, :, 0:ncols], w_up_b.t[:, c0:c0 + ncols].rearrange("(k p) c -> p k c", p=128), sb=wu_, r=[w_up_b], w=[wu_])
                for jb in range(ncols // 128):
                    fi = c0 // 128 + jb
                    Gb, Ub = FB[(2 * n_f) % 6], FB[(2 * n_f + 1) % 6]
                    gs_, tm_, tm2_, ast_ = gs[n_f % 2], tm[n_f % 2], tm2[n_f % 2], ast[n_f % 3]
                    n_f += 1

                    def fg(Gb=Gb, jb=jb, wg_=wg_, nt=nt):
                        inst = None
                        for k in range(KD):
                            inst = nc.tensor.matmul(Gb.t[:, 0:nt], lhsT=wg_.t[:, k, jb * 128:(jb + 1) * 128], rhs=h2c.t[:, k, 0:nt], start=(k == 0), stop=(k == KD - 1))
                        return inst
                    kb.op(PE, fg, r=[wg_, h2c], w=[Gb])
                    if halo:
                        kb.op(ACT, lambda Gb=Gb, fi=fi, nt=nt: nc.scalar.copy(out=gh.t[:, fi, :], in_=Gb.t[:, nt - 2:nt]), r=[Gb], p=[gh])
                        continue

                    def fu(Ub=Ub, jb=jb, wu_=wu_, nt=nt):
                        inst = None
                        for k in range(KD):
                            inst = nc.tensor.matmul(Ub.t[:, 0:nt], lhsT=wu_.t[:, k, jb * 128:(jb + 1) * 128], rhs=h2c.t[:, k, 0:nt], start=(k == 0), stop=(k == KD - 1))
                        return inst
                    kb.op(PE, fu, r=[wu_, h2c], w=[Ub])
                    kb.op(DVE, lambda gs_=gs_, fi=fi: nc.vector.tensor_copy(out=gs_.t[:, 0:2], in_=gh.t[:, fi, :]), r=[gh], w=[gs_])
                    kb.op(ACT, lambda gs_=gs_, Gb=Gb, nt=nt: nc.scalar.copy(out=gs_.t[:, 2:2 + nt], in_=Gb.t[:, 0:nt]), r=[Gb], p=[gs_])
                    kb.op(DVE, lambda gs_=gs_, fi=fi, nt=nt: nc.vector.tensor_copy(out=gh.t[:, fi, :], in_=gs_.t[:, nt:nt + 2]), r=[gs_], w=[gh])
                    kb.op(ACT, lambda tm_=tm_, Gb=Gb, fi=fi, nt=nt: nc.scalar.activation(out=tm_.t[:, 0:nt], in_=Gb.t[:, 0:nt], func=AF.Identity, scale=cw.t[:, 2, fi:fi + 1], bias=cbs.t[:, fi:fi + 1]), r=[Gb, cw, cbs], w=[tm_])
                    kb.op(DVE, lambda tm_=tm_, gs_=gs_, fi=fi, nt=nt: nc.vector.scalar_tensor_tensor(out=tm_.t[:, 0:nt], in0=gs_.t[:, 1:1 + nt], scalar=cw.t[:, 1, fi:fi + 1], in1=tm_.t[:, 0:nt], op0=ALU.mult, op1=ALU.add), r=[gs_, cw], w=[tm_])
                    kb.op(DVE, lambda tm_=tm_, gs_=gs_, fi=fi, nt=nt: nc.vector.scalar_tensor_tensor(out=tm_.t[:, 0:nt], in0=gs_.t[:, 0:nt], scalar=cw.t[:, 0, fi:fi + 1], in1=tm_.t[:, 0:nt], op0=ALU.mult, op1=ALU.add), r=[gs_, cw], w=[tm_])
                    kb.op(ACT, lambda tm_=tm_, tm2_=tm2_, nt=nt: nc.scalar.activation(out=tm2_.t[:, 0:nt], in_=tm_.t[:, 0:nt], func=AF.Silu), r=[tm_], w=[tm2_])
                    kb.op(DVE, lambda tm2_=tm2_, Ub=Ub, ast_=ast_, nt=nt: nc.vector.tensor_tensor(out=ast_.t[:, 0:nt], in0=tm2_.t[:, 0:nt], in1=Ub.t[:, 0:nt], op=ALU.mult), r=[tm2_, Ub], w=[ast_])
                    kb.dma(POOL, actT_s.t[fi, :, t0:t0 + nt], ast_.t[:, 0:nt], sb=ast_, r=[ast_], p=[actT_s])
                c0 += ncols
        kb.barrier(ph + allbufs + FB + BB)
        kb.retire(ph)

    with ExitStack() as es:
        def sb(name, shape, dt):
            return Buf(name, es.enter_context(nc.sbuf_tensor(kb.name(name), list(shape), dt)))
        actc = sb("actc", [128, NF, 512], BF16)
        wd = [sb("wd", [128, 8, 512], BF16) for _ in range(4)]
        xmp = [sb("xmp", [128, 512], F32) for _ in range(4)]
        ost = [sb("ost", [128, 512], F32) for _ in range(4)]
        ph = [actc] + wd + xmp + ost
        n_w = n_e = 0
        for (t0, nt) in qchunks[1:]:
            f0 = 0
            while f0 < NF:
                f1 = min(NF, f0 + 16)
                kb.dma(SP, actc.t[:, f0:f1, :], actT_s.t[f0:f1, :, t0:t0 + 512].rearrange("k p t -> p k t"), sb=actc, r=[actT_s], w=[actc] if f0 == 0 else (), p=() if f0 == 0 else [actc])
                f0 = f1
            for cb in range(D // 512):
                accs = FB[0:4]
                for kg in range((NF + 7) // 8):
                    nk = min(8, NF - kg * 8)
                    wd_ = wd[n_w % 4]
                    n_w += 1
                    kb.dma(SP, wd_.t[:, 0:nk, :], w_down_b.t[kg * 1024:kg * 1024 + nk * 128, cb * 512:(cb + 1) * 512].rearrange("(k p) c -> p k c", p=128), sb=wd_, r=[w_down_b], w=[wd_])

                    def f(kg=kg, nk=nk, wd_=wd_, accs=accs):
                        inst = None
                        for kk in range(nk):
                            fi = kg * 8 + kk
                            for sub in range(4):
                                inst = nc.tensor.matmul(accs[sub].t[:], lhsT=actc.t[:, fi, sub * 128:(sub + 1) * 128], rhs=wd_.t[:, kk, :], start=(fi == 0), stop=(fi == NF - 1))
                        return inst
                    if kg == 0:
                        kb.op(PE, f, r=[actc, wd_], w=accs)
                    else:
                        kb.op(PE, f, r=[actc, wd_], p=accs)
                for sub in range(4):
                    xp, os_ = xmp[n_e % 4], ost[n_e % 4]
                    n_e += 1
                    r0 = t0 + sub * 128
                    kb.dma(SP, xp.t[:], xmid_s.t[r0:r0 + 128, cb * 512:(cb + 1) * 512], sb=xp, r=[xmid_s], w=[xp])
                    kb.op(DVE, lambda sub=sub, xp=xp, os_=os_, accs=accs: nc.vector.tensor_tensor(out=os_.t[:], in0=accs[sub].t[:], in1=xp.t[:], op=ALU.add), r=[accs[sub], xp], w=[os_])
                    kb.dma(POOL, y_out.t[r0 - HALF:r0 - HALF + 128, cb * 512:(cb + 1) * 512], os_.t[:], sb=os_, r=[os_], p=[y_out])
        kb.barrier(ph + allbufs + FB + BB)
        kb.retire(ph)
    return nc


def prep_shared(cfg, inp):
    c = derive(cfg)
    D, H, G, F = c["D"], c["H"], c["G"], c["F"]
    SW, NST, NKC, NF = c["SW"], c["NST"], c["NKC"], c["NF"]
    f32 = np.float32
    rep = lambda v, n=128: np.ascontiguousarray(np.broadcast_to(np.asarray(v, f32).reshape(1, -1), (n, np.asarray(v).size)))
    d = {}
    d["anorm"] = rep(inp["attn_norm"][0])
    d["fnorm"] = rep(inp["ffn_norm"][0])
    d["w_in"] = np.ascontiguousarray(inp["w_in"][0], f32)
    d["w_out"] = np.ascontiguousarray(inp["w_out"][0], f32)
    d["w_gate"] = np.ascontiguousarray(inp["w_gate"][0], f32)
    d["w_up"] = np.ascontiguousarray(inp["w_up"][0], f32)
    d["w_down"] = np.ascontiguousarray(inp["w_down"][0], f32)
    d["w_glu"] = np.ascontiguousarray(inp["ssm_w_glu"][0], f32)
    d["qg"] = np.ascontiguousarray(np.asarray(inp["q_gain"][0], f32).reshape(128, 1))
    d["kg"] = np.ascontiguousarray(np.asarray(inp["k_gain"][0], f32).reshape(128, 1))
    lam = np.stack([inp["lam_q1"][0], inp["lam_k1"][0], inp["lam_q2"][0], inp["lam_k2"][0]]).astype(f32)
    d["lamv"] = np.ascontiguousarray(np.broadcast_to(lam[None], (128, 4, 128)))
    d["subg"] = rep(inp["sub_gain"][0])
    a_re = np.asarray(inp["ssm_a_re"][0], f32)
    a_im = np.asarray(inp["ssm_a_im"][0], f32)
    ldt = np.asarray(inp["ssm_log_dt"][0], f32)
    st_l = lambda a: np.ascontiguousarray(a.reshape(NST, 2, 64).transpose(1, 2, 0).reshape(128, NST))
    d["sAre"] = st_l(a_re)
    d["sAim"] = st_l(a_im)
    d["sLdt"] = st_l(np.broadcast_to(ldt[:, None], (G, 64)))
    ch_l = lambda a: np.ascontiguousarray(np.broadcast_to(a.reshape(NKC, 8, 1, 64), (NKC, 8, 16, 64)).transpose(1, 2, 0, 3).reshape(128, NKC * 64))
    d["cAre"] = ch_l(a_re)
    d["cAim"] = ch_l(a_im)
    d["cLdt"] = ch_l(np.broadcast_to(ldt[:, None], (G, 64)))
    bt = lambda b: np.ascontiguousarray(np.asarray(b, f32).reshape(NKC, 8, 64, 16).transpose(1, 3, 0, 2).reshape(128, NKC * 64))
    d["BTre"] = bt(inp["ssm_b_re"][0])
    d["BTim"] = bt(inp["ssm_b_im"][0])
    ct = lambda cc: np.ascontiguousarray(np.asarray(cc, f32).reshape(NST, 2, 16, 64).transpose(1, 3, 0, 2).reshape(128, NST, 16))
    d["CTre"] = ct(inp["ssm_c_re"][0])
    d["CTim"] = ct(inp["ssm_c_im"][0])
    pk = lambda v: np.ascontiguousarray(np.asarray(v, f32).reshape(NKC, 128).T)
    d["dsk"] = pk(inp["ssm_d"][0])
    d["bglu"] = pk(inp["ssm_b_glu"][0])
    d["outg"] = pk(inp["ssm_out_gain"][0])
    d["convw"] = np.ascontiguousarray(np.asarray(inp["conv_w"][0], f32).reshape(3, NF, 128).transpose(2, 0, 1))
    d["convb"] = np.ascontiguousarray(np.asarray(inp["conv_b"][0], f32).reshape(NF, 128).T)
    bf = ml_dtypes.bfloat16
    d["ident"] = np.eye(128, dtype=f32).astype(bf)
    kk = np.arange(128)
    d["cmask"] = (kk[:, None] <= kk[None, :]).astype(f32).astype(bf)
    d["onesb"] = np.ones((128, 128), f32).astype(bf)
    d["negm"] = (-30000.0 * (kk[None, :] > kk[:, None])).astype(f32).astype(bf)
    d["iota_t"] = np.ascontiguousarray(np.broadcast_to(np.arange(128, dtype=f32)[None], (128, 128)))
    rm = np.zeros((128, 8), f32)
    for j in range(8):
        rm[j * 16:(j + 1) * 16, j] = 1.0
    d["rowmask"] = rm
    return d


def run(cfg, inp, trace=False):
    c = derive(cfg)
    HALF, D = c["HALF"], c["D"]
    x = np.asarray(inp["x"], np.float32)
    B = x.shape[0]
    assert x.shape[1] == 2 * HALF
    shared = prep_shared(cfg, inp)
    in_maps = []
    for core in range(2 * B):
        b, r = core // 2, core % 2
        m = dict(shared)
        xe = np.zeros((2 * HALF, D), np.float32)
        if r == 1:
            xe[:] = x[b]
        else:
            xe[HALF:] = x[b, :HALF]
        m["x_ext"] = xe
        m["flag"] = np.full((128, 1), float(r), np.float32)
        in_maps.append(m)
    nc = build_nc(cfg)
    res = run_bass_kernel_spmd(nc, in_maps, core_ids=list(range(2 * B)), **({"trace": True} if trace else {}))
    out = np.empty((B, 2 * HALF, D), np.float32)
    for core in range(2 * B):
        b, r = core // 2, core % 2
        out[b, r * HALF:(r + 1) * HALF] = res.results[core]["y"]
    return out, res


def kernel(**inputs):
    out, _ = run(CFG_FULL, inputs)
    return out
```
